# Optimizing a Trainium2 kernel written in Bass

```python
import math
import jax, jax.numpy as jnp
from jax import lax
import numpy as np

D_MODEL = 2048
BATCH = 4
SEQ = 4096
DEPTH = 2

MIX_WIDTH = D_MODEL
RET_HEADS = 4
RET_WIDTH = MIX_WIDTH // 4
RET_DIM = RET_WIDTH // RET_HEADS
RET_CHUNK = 128
RET_ROPE_THETA = 10000.0
DIFF_HEADS = 4
DIFF_WIDTH = MIX_WIDTH // 2
DIFF_V_DIM = DIFF_WIDTH // DIFF_HEADS
DIFF_QK_DIM = DIFF_V_DIM // 2
DIFF_ROT_DIM = DIFF_QK_DIM // 4
ROPE_THETA = 500000.0
Q_BLOCK = 128
MLSTM_HEADS = 4
MLSTM_WIDTH = MIX_WIDTH // 4
MLSTM_DIM = MLSTM_WIDTH // MLSTM_HEADS
MLSTM_CHUNK = 64
CONV_K = 4
D_FF = ((8 * D_MODEL // 3 + 127) // 128) * 128
ALPHA = (2 * DEPTH) ** 0.25
BETA = (8 * DEPTH) ** -0.25
PROJ_SPLITS = (
    RET_WIDTH, RET_WIDTH, RET_WIDTH, RET_WIDTH,
    2 * DIFF_HEADS * DIFF_QK_DIM, 2 * DIFF_HEADS * DIFF_QK_DIM,
    DIFF_WIDTH,
    MLSTM_WIDTH, MLSTM_WIDTH, MLSTM_WIDTH, MLSTM_WIDTH,
    MLSTM_HEADS, MLSTM_HEADS,
)
PROJ_WIDTH = sum(PROJ_SPLITS)
PROJ_OFFSETS = tuple(int(o) for o in np.cumsum(PROJ_SPLITS)[:-1])

kernel_name = "hymba_style_retention_diffattn_mlstm_macaron_deepnorm"

F32 = jnp.float32


def layer_norm(x, g, b, eps=1e-5):
    xf = x.astype(F32)
    mu = jnp.mean(xf, axis=-1, keepdims=True)
    xc = xf - mu
    var = jnp.mean(xc * xc, axis=-1, keepdims=True)
    return (xc * lax.rsqrt(var + eps) * g + b).astype(x.dtype)


def head_norm(t, g, rms, eps=1e-6):
    if rms:
        y = t * lax.rsqrt(jnp.mean(t * t, axis=-1, keepdims=True) + eps)
    else:
        tc = t - jnp.mean(t, axis=-1, keepdims=True)
        y = tc * lax.rsqrt(jnp.mean(tc * tc, axis=-1, keepdims=True) + eps)
    return y * g


def swiglu(x, w_gu, w_down):
    a, u = jnp.split(x @ w_gu, 2, axis=-1)
    return (jax.nn.silu(a) * u) @ w_down


def rope_tables(seq, rot_dim, theta):
    pos = jnp.arange(seq, dtype=F32)
    inv = theta ** (-jnp.arange(0, rot_dim, 2, dtype=F32) / rot_dim)
    ang = pos[:, None] * inv[None, :]
    return jnp.cos(ang), jnp.sin(ang)


def apply_rotary(t, cos, sin, rot_dim):
    half = rot_dim // 2
    t1, t2, tp = t[..., :half], t[..., half:rot_dim], t[..., rot_dim:]
    return jnp.concatenate([t1 * cos - t2 * sin, t1 * sin + t2 * cos, tp], axis=-1).astype(t.dtype)


def split_heads(t, n_heads):
    B, S, W = t.shape
    return t.reshape(B, S, n_heads, W // n_heads).transpose(0, 2, 1, 3)


def merge_heads(t):
    B, H, S, d = t.shape
    return t.transpose(0, 2, 1, 3).reshape(B, S, H * d)


def to_chunks(t, c):
    B, H, S = t.shape[:3]
    t = t.reshape((B, H, S // c, c) + t.shape[3:])
    return jnp.moveaxis(t, 2, 0)


def from_chunks(t):
    n, B, H, c, d = t.shape
    return jnp.moveaxis(t, 0, 2).reshape(B, H, n * c, d)


def retention_chunkwise(q, k, v):
    B, H, S, d = q.shape
    C = RET_CHUNK
    k = k * (d ** -0.5)
    log_g = jnp.log(1.0 - jnp.exp2(-5.0 - jnp.arange(H, dtype=F32)))
    idx = jnp.arange(C, dtype=F32)
    rel = idx[:, None] - idx[None, :]
    decay_in = jnp.where(rel >= 0, jnp.exp(jnp.maximum(rel, 0.0) * log_g[:, None, None]), 0.0)
    xi = jnp.exp((idx + 1.0)[None, :] * log_g[:, None])[None, :, :, None]
    zeta = jnp.exp((C - 1.0 - idx)[None, :] * log_g[:, None])[None, :, :, None]
    g_chunk = jnp.exp(C * log_g)[None, :, None, None]

    def step(R, inp):
        qc, kc, vc = inp
        inner = jnp.einsum('bhid,bhjd->bhij', qc, kc) * decay_in
        y = (jnp.einsum('bhij,bhjv->bhiv', inner, vc)
             + jnp.einsum('bhid,bhdv->bhiv', qc, R) * xi)
        R = g_chunk * R + jnp.einsum('bhjd,bhjv->bhdv', kc * zeta, vc)
        return R, y

    R0 = jnp.zeros((B, H, d, v.shape[-1]), F32)
    _, ys = lax.scan(step, R0, (to_chunks(q, C), to_chunks(k, C), to_chunks(v, C)))
    return from_chunks(ys)


def diff_attention(q, k, v, lam):
    B, H, _, S, dq = q.shape
    nb = S // Q_BLOCK
    scale = dq ** -0.5
    qb = jnp.moveaxis(q.reshape(B, H, 2, nb, Q_BLOCK, dq), 3, 0)
    kpos = jnp.arange(S)
    vf = v.astype(F32)

    def block(args):
        qblk, bi = args
        s = jnp.einsum('bhmqd,bhmkd->bhmqk', qblk, k).astype(F32) * scale
        qpos = bi * Q_BLOCK + jnp.arange(Q_BLOCK)
        s = jnp.where(kpos[None, :] <= qpos[:, None], s, -jnp.inf)
        p = jax.nn.softmax(s, axis=-1)
        a = p[:, :, 0] - lam * p[:, :, 1]
        return jnp.einsum('bhqk,bhkv->bhqv', a, vf)

    o = lax.map(block, (qb, jnp.arange(nb)))
    return from_chunks(o)


def mlstm_chunkwise(q, k, v, i_pre, f_pre):
    B, H, S, d = q.shape
    L = MLSTM_CHUNK
    k = k * (d ** -0.5)
    log_f = jax.nn.log_sigmoid(f_pre)
    tril = jnp.tril(jnp.ones((L, L), dtype=bool))

    def step(carry, inp):
        Cs, ns, m = carry
        qc, kc, vc, ic, lfc = inp
        b = jnp.cumsum(lfc, axis=-1)
        a = b + m[..., None]
        Dm = jnp.where(tril, b[..., :, None] - b[..., None, :] + ic[..., None, :], -jnp.inf)
        mt = jnp.maximum(a, jnp.max(Dm, axis=-1))
        sc = jnp.einsum('bhid,bhjd->bhij', qc, kc) * jnp.exp(Dm - mt[..., None])
        inter = jnp.exp(a - mt)
        num = (jnp.einsum('bhij,bhjv->bhiv', sc, vc)
               + inter[..., None] * jnp.einsum('bhid,bhdv->bhiv', qc, Cs))
        den = jnp.sum(sc, axis=-1) + inter * jnp.einsum('bhid,bhd->bhi', qc, ns)
        h = num / jnp.maximum(jnp.abs(den), jnp.exp(-mt))[..., None]
        bL = b[..., -1]
        w_log = bL[..., None] - b + ic
        m_new = jnp.maximum(bL + m, jnp.max(w_log, axis=-1))
        dec = jnp.exp(bL + m - m_new)
        kw = kc * jnp.exp(w_log - m_new[..., None])[..., None]
        Cs = dec[..., None, None] * Cs + jnp.einsum('bhjd,bhjv->bhdv', kw, vc)
        ns = dec[..., None] * ns + jnp.sum(kw, axis=2)
        return (Cs, ns, m_new), h

    init = (jnp.zeros((B, H, d, d), F32), jnp.zeros((B, H, d), F32), jnp.zeros((B, H), F32))
    _, hs = lax.scan(step, init, (to_chunks(q, L), to_chunks(k, L), to_chunks(v, L),
                                  to_chunks(i_pre, L), to_chunks(log_f, L)))
    return from_chunks(hs)


def causal_dwconv_silu(u, w, b):
    S = u.shape[1]
    up = jnp.pad(u, ((0, 0), (CONV_K - 1, 0), (0, 0)))
    y = b + sum(up[:, j:j + S, :] * w[j] for j in range(CONV_K))
    return jax.nn.silu(y)


def token_mixer(h, w_in, ret_norm_g, diff_lambda, diff_norm_g, conv_w, conv_b, gate_b,
                mlstm_norm_g, w_out, lam_init, ret_cs, diff_cs):
    B, S, _ = h.shape
    proj = h @ w_in
    (rq, rk, rv, rg, dq, dk, dv, mq, mk, mv, mo, mi, mf) = jnp.split(proj, PROJ_OFFSETS, axis=-1)

    rcos, rsin = ret_cs
    rq = apply_rotary(split_heads(rq, RET_HEADS).astype(F32), rcos, rsin, RET_DIM)
    rk = apply_rotary(split_heads(rk, RET_HEADS).astype(F32), rcos, rsin, RET_DIM)
    ry = retention_chunkwise(rq, rk, split_heads(rv, RET_HEADS).astype(F32))
    ry = head_norm(ry, ret_norm_g.reshape(RET_HEADS, 1, RET_DIM), rms=False)
    ret_out = jax.nn.silu(rg.astype(F32)) * merge_heads(ry)

    dcos, dsin = diff_cs
    def qk_heads(t):
        t = t.reshape(B, S, DIFF_HEADS, 2, DIFF_QK_DIM).transpose(0, 2, 3, 1, 4)
        return apply_rotary(t.astype(F32), dcos, dsin, DIFF_ROT_DIM)
    dl = diff_lambda.astype(F32)
    lam = (jnp.exp(jnp.sum(dl[0] * dl[1])) - jnp.exp(jnp.sum(dl[2] * dl[3])) + lam_init)
    dy = diff_attention(qk_heads(dq), qk_heads(dk), split_heads(dv, DIFF_HEADS), lam)
    dy = head_norm(dy, diff_norm_g, rms=True) * (1.0 - lam_init)
    diff_out = merge_heads(dy)

    mqk = causal_dwconv_silu(jnp.concatenate([mq, mk], axis=-1), conv_w, conv_b)
    mq_c, mk_c = jnp.split(mqk.astype(F32), 2, axis=-1)
    i_pre = (mi.astype(F32) + gate_b[0]).transpose(0, 2, 1)
    f_pre = (mf.astype(F32) + gate_b[1]).transpose(0, 2, 1)
    mh = mlstm_chunkwise(split_heads(mq_c, MLSTM_HEADS), split_heads(mk_c, MLSTM_HEADS),
                         split_heads(mv, MLSTM_HEADS).astype(F32), i_pre, f_pre)
    mh = jax.nn.sigmoid(split_heads(mo, MLSTM_HEADS).astype(F32)) * mh
    mh = head_norm(mh, mlstm_norm_g.reshape(MLSTM_HEADS, 1, MLSTM_DIM), rms=False)
    mlstm_out = merge_heads(mh)

    mixed = jnp.concatenate([ret_out, diff_out, mlstm_out], axis=-1).astype(h.dtype)
    return mixed @ w_out


def setup_inputs(seed: int = 0) -> dict:
    key = jax.random.key(seed)
    ks = jax.random.split(key, 20)
    nrm = lambda k, shape, s: jax.random.normal(k, shape, F32) * s
    gate_b = jnp.stack([
        nrm(ks[12], (DEPTH, MLSTM_HEADS), 0.1),
        jnp.linspace(3.0, 6.0, MLSTM_HEADS, dtype=F32)[None, :] + nrm(ks[13], (DEPTH, MLSTM_HEADS), 0.1),
    ], axis=1)
    return {
        "x": nrm(ks[0], (BATCH, SEQ, D_MODEL), 1.0),
        "ln_g": 1.0 + nrm(ks[1], (DEPTH, 3, D_MODEL), 0.02),
        "ln_b": nrm(ks[2], (DEPTH, 3, D_MODEL), 0.02),
        "ffn1_w_gu": nrm(ks[3], (DEPTH, D_MODEL, 2 * D_FF), D_MODEL ** -0.5),
        "ffn1_w_down": nrm(ks[4], (DEPTH, D_FF, D_MODEL), BETA * D_FF ** -0.5),
        "w_in": nrm(ks[5], (DEPTH, D_MODEL, PROJ_WIDTH), D_MODEL ** -0.5),
        "ret_norm_g": 1.0 + nrm(ks[6], (DEPTH, RET_WIDTH), 0.02),
        "diff_lambda": nrm(ks[7], (DEPTH, 4, DIFF_QK_DIM), 0.1),
        "diff_norm_g": 1.0 + nrm(ks[8], (DEPTH, DIFF_V_DIM), 0.02),
        "mlstm_conv_w": nrm(ks[9], (DEPTH, CONV_K, 2 * MLSTM_WIDTH), CONV_K ** -0.5),
        "mlstm_conv_b": nrm(ks[10], (DEPTH, 2 * MLSTM_WIDTH), 0.02),
        "mlstm_gate_b": gate_b,
        "mlstm_norm_g": 1.0 + nrm(ks[11], (DEPTH, MLSTM_WIDTH), 0.02),
        "w_out": nrm(ks[14], (DEPTH, MIX_WIDTH, D_MODEL), BETA * MIX_WIDTH ** -0.5),
        "ffn2_w_gu": nrm(ks[15], (DEPTH, D_MODEL, 2 * D_FF), D_MODEL ** -0.5),
        "ffn2_w_down": nrm(ks[16], (DEPTH, D_FF, D_MODEL), BETA * D_FF ** -0.5),
    }


def reference(x, ln_g, ln_b, ffn1_w_gu, ffn1_w_down, w_in, ret_norm_g, diff_lambda, diff_norm_g,
              mlstm_conv_w, mlstm_conv_b, mlstm_gate_b, mlstm_norm_g, w_out, ffn2_w_gu, ffn2_w_down):
    S = x.shape[1]
    ret_cs = rope_tables(S, RET_DIM, RET_ROPE_THETA)
    diff_cs = rope_tables(S, DIFF_ROT_DIM, ROPE_THETA)
    for l in range(DEPTH):
        lam_init = 0.8 - 0.6 * math.exp(-0.3 * l)
        x = layer_norm(ALPHA * x + 0.5 * swiglu(x, ffn1_w_gu[l], ffn1_w_down[l]), ln_g[l, 0], ln_b[l, 0])
        x = layer_norm(ALPHA * x + token_mixer(x, w_in[l], ret_norm_g[l], diff_lambda[l], diff_norm_g[l],
                                               mlstm_conv_w[l], mlstm_conv_b[l], mlstm_gate_b[l],
                                               mlstm_norm_g[l], w_out[l], lam_init, ret_cs, diff_cs),
                       ln_g[l, 1], ln_b[l, 1])
        x = layer_norm(ALPHA * x + 0.5 * swiglu(x, ffn2_w_gu[l], ffn2_w_down[l]), ln_g[l, 2], ln_b[l, 2])
    return x
```

```python
import contextlib
import numpy as np
import concourse.bass as bass
import concourse.mybir as mybir
from concourse.bass_utils import run_bass_kernel_spmd

F32 = mybir.dt.float32
BF16 = mybir.dt.bfloat16
AF = mybir.ActivationFunctionType
ALU = mybir.AluOpType
AX = mybir.AxisListType


class T:
    __slots__ = ("ap", "key", "excl")

    def __init__(self, ap, key, excl=False):
        self.ap = ap
        self.key = key
        self.excl = excl

    def __getitem__(self, idx):
        return T(self.ap[idx], self.key, self.excl)

    def k(self, sub):
        return T(self.ap, (self.key, sub), self.excl)


class Op:
    __slots__ = ("eng", "fn", "deps", "sig", "tok", "pos", "dma", "waits", "gidx", "cc")


class Prog:
    ENGS = ("pe", "act", "dve", "pool", "sp")
    RING = 8
    SAMEW = 2

    def __init__(self, nc):
        self.nc = nc
        self.ops = []
        self.lastw = {}
        self.readers = {}
        self.stack = contextlib.ExitStack()
        self.nalloc = 0
        self.last_on = {}
        self.dma_hist = {e: [] for e in self.ENGS}
        self.pending = {}

    def fence(self):
        F = set(self.last_on.values())
        for e in self.ENGS:
            F.update(self.dma_hist[e][-self.RING:])
        self.pending = {e: set(F) for e in self.ENGS}

    def sbuf(self, name, shape, dt):
        h = self.stack.enter_context(self.nc.sbuf_tensor(name, list(shape), dt))
        return T(h[:], name)

    def psum(self, name, shape, dt=F32):
        h = self.stack.enter_context(self.nc.psum_tensor(name, list(shape), dt))
        return T(h[:], name, True)

    def dram(self, name, shape, dt, kind="Internal"):
        h = self.nc.dram_tensor(name, list(shape), dt, kind=kind)
        return T(h.ap(), name if kind != "ExternalInput" else None)

    def op(self, eng, fn, r=(), w=(), dma=False):
        o = Op()
        o.eng, o.fn, o.dma, o.sig, o.tok, o.waits = eng, fn, dma, False, None, None
        o.cc = False
        o.gidx = len(self.ops)
        deps = set()
        w = list(w) + [t for t in r if t.excl]
        for t in r:
            k = t.key
            if k is None:
                continue
            lw = self.lastw.get(k)
            if lw is not None:
                deps.add(lw)
            self.readers.setdefault(k, []).append(o)
        for t in w:
            k = t.key
            if k is None:
                continue
            lw = self.lastw.get(k)
            if lw is not None:
                deps.add(lw)
            for rd in self.readers.get(k, ()):
                if rd is not o:
                    deps.add(rd)
            self.readers[k] = []
            self.lastw[k] = o
        pend = self.pending.pop(eng, None)
        if pend:
            deps |= pend
        deps.discard(o)
        o.deps = deps
        self.ops.append(o)
        self.last_on[eng] = o
        if dma:
            self.dma_hist[eng].append(o)
        return o

    def mm(self, out, lhsT, rhs, start=True, stop=True):
        r = [lhsT, rhs] + ([] if start else [out])
        self.op("pe", lambda e: e.matmul(out.ap, lhsT.ap, rhs.ap, start=start, stop=stop), r, [out])

    def tr(self, out, in_, ident):
        self.op("pe", lambda e: e.transpose(out.ap, in_.ap, ident.ap), [in_, ident], [out])

    def act(self, out, in_, func, bias=None, scale=None, accum_out=None, eng="act"):
        kw = {}
        r = [in_]
        w = [out]
        if bias is not None:
            if isinstance(bias, T):
                kw["bias"] = bias.ap
                r.append(bias)
            else:
                kw["bias"] = bias
        if scale is not None:
            if isinstance(scale, T):
                kw["scale"] = scale.ap
                r.append(scale)
            else:
                kw["scale"] = scale
        if accum_out is not None:
            kw["accum_out"] = accum_out.ap
            w.append(accum_out)
        self.op("act", lambda e: e.activation(out.ap, in_.ap, func, **kw), r, w)

    def tt(self, eng, out, in0, in1, op):
        self.op(eng, lambda e: e.tensor_tensor(out.ap, in0.ap, in1.ap, op), [in0, in1], [out])

    def ts(self, eng, out, in0, s1, s2=None, op0=ALU.mult, op1=None, accum_out=None):
        r = [in0]
        a1 = s1.ap if isinstance(s1, T) else s1
        a2 = s2.ap if isinstance(s2, T) else s2
        if isinstance(s1, T):
            r.append(s1)
        if isinstance(s2, T):
            r.append(s2)
        w = [out]
        kw = {}
        if op1 is not None:
            kw["op1"] = op1
        if accum_out is not None:
            kw["accum_out"] = accum_out.ap
            w.append(accum_out)
        self.op(eng, lambda e: e.tensor_scalar(out.ap, in0.ap, a1, a2, op0, **kw), r, w)

    def stt(self, eng, out, in0, scalar, in1, op0, op1):
        r = [in0, in1]
        a = scalar.ap if isinstance(scalar, T) else scalar
        if isinstance(scalar, T):
            r.append(scalar)
        self.op(eng, lambda e: e.scalar_tensor_tensor(out.ap, in0.ap, a, in1.ap, op0, op1), r, [out])

    def rsum(self, out, in_):
        self.op("dve", lambda e: e.reduce_sum(out.ap, in_.ap, AX.X), [in_], [out])

    def rmax(self, out, in_):
        self.op("dve", lambda e: e.reduce_max(out.ap, in_.ap, AX.X), [in_], [out])

    def scan(self, out, d0, d1, initial, op0, op1):
        r = [d0, d1]
        a = initial.ap if isinstance(initial, T) else initial
        if isinstance(initial, T):
            r.append(initial)
        self.op("dve", lambda e: e.tensor_tensor_scan(out.ap, d0.ap, d1.ap, a, op0, op1), r, [out])

    def recip(self, out, in_):
        self.op("dve", lambda e: e.reciprocal(out.ap, in_.ap), [in_], [out])

    def copy(self, eng, out, in_):
        if eng == "act":
            self.op("act", lambda e: e.activation(out.ap, in_.ap, AF.Copy), [in_], [out])
        else:
            self.op(eng, lambda e: e.tensor_copy(out.ap, in_.ap), [in_], [out])

    def memset(self, eng, out, val):
        self.op(eng, lambda e: e.memset(out.ap, val), [], [out])

    def cc(self, kind, groups, out, in_):
        o = self.op("pool", lambda e: e.collective_compute(kind, ALU.bypass, replica_groups=groups,
                                                            ins=[in_.ap.opt()], outs=[out.ap.opt()]),
                    [in_], [out], dma=True)
        o.cc = True
        return o

    def dma(self, q, out, in_, rkeys=None, wkeys=None, **kw):
        self.op(q, lambda e: e.dma_start(out=out.ap, in_=in_.ap, **kw),
                rkeys if rkeys is not None else [in_], wkeys if wkeys is not None else [out], dma=True)

    def emit(self):
        nc = self.nc
        st = self.stack
        per = {e: [] for e in self.ENGS}
        for o in self.ops:
            o.pos = len(per[o.eng])
            per[o.eng].append(o)

        def need(o, d):
            if d.dma:
                return True
            if d.eng != o.eng:
                return True
            if o.eng == "pe" and not o.dma:
                return False
            if o.dma:
                return True
            return (o.pos - d.pos) <= self.SAMEW

        for o in self.ops:
            o.waits = [d for d in o.deps if need(o, d)]
            for d in o.waits:
                d.sig = True
        esem = {e: st.enter_context(nc.semaphore("es_" + e)) for e in self.ENGS}
        rings = {}
        ecount = {e: 0 for e in self.ENGS}
        dcount = {e: 0 for e in self.ENGS}
        for e in self.ENGS:
            nd = sum(1 for o in per[e] if o.dma and not o.cc)
            if nd:
                rings[e] = [st.enter_context(nc.semaphore("ds_%s_%d" % (e, i)))
                            for i in range(min(self.RING, nd))]
        for e in self.ENGS:
            for o in per[e]:
                if o.cc:
                    o.tok = (st.enter_context(nc.semaphore("cc_%d" % o.gidx)), 1, 1, None)
                elif o.dma:
                    j = dcount[e]
                    dcount[e] += 1
                    R = len(rings[e])
                    o.tok = (rings[e][j % R], 16 * (j // R + 1), 16, j)
                elif o.sig:
                    ecount[e] += 1
                    o.tok = (esem[e], ecount[e], 1, None)

        def run(ename, eng):
            known = {}

            def wait(sem, val):
                if known.get(id(sem), 0) >= val:
                    return
                known[id(sem)] = val
                eng.wait_ge(sem, val)

            for o in per[ename]:
                if getattr(self, "dump", None) is not None:
                    self.dump.append((ename, o.gidx, o.dma, [(d.eng, d.gidx, d.dma, d.tok[1]) for d in sorted(o.waits, key=lambda d: d.gidx)], None if o.tok is None else o.tok[1]))
                for d in sorted(o.waits, key=lambda d: d.gidx):
                    wait(d.tok[0], d.tok[1])
                if o.dma and not o.cc:
                    sem, val, inc, j = o.tok
                    if val > 16:
                        wait(sem, val - 16)
                ins = o.fn(eng)
                if o.tok is not None:
                    ins.then_inc(o.tok[0], o.tok[2])
                if o.cc:
                    wait(o.tok[0], 1)
            if ename in rings:
                R = len(rings[ename])
                n = dcount[ename]
                for i in range(R):
                    cnt = (n - i + R - 1) // R if n > i else 0
                    if cnt:
                        wait(rings[ename][i], 16 * cnt)

        with nc.Block() as block:
            if per["pe"]:
                @block.tensor
                def _(e):
                    run("pe", e)
            if per["act"]:
                @block.scalar
                def _(e):
                    run("act", e)
            if per["dve"]:
                @block.vector
                def _(e):
                    run("dve", e)
            if per["pool"]:
                @block.gpsimd
                def _(e):
                    run("pool", e)
            if per["sp"]:
                @block.sync
                def _(e):
                    run("sp", e)
        self.stack.close()
        return {e: len(per[e]) for e in self.ENGS}
import math

D = 2048
FF = 5504
NFC = FF // 128
NDC = D // 128
TB = 512
DEPTH = 2
ALPHA = (2 * DEPTH) ** 0.25
LN_EPS = 1e-5


class Arena:
    def __init__(self, P, nbytes):
        self.P = P
        self.t = P.sbuf("arena", [128, nbytes // 4], F32)
        self.off = 0
        self.n = 0
        self.nbytes = nbytes

    def reset(self):
        self.off = 0

    def alloc(self, name, free_shape, dt):
        esz = 4 if dt == F32 else 2
        n = 1
        for s in free_shape:
            n *= s
        nb = (n * esz + 63) // 64 * 64
        assert self.off + nb <= self.nbytes, (name, self.off, nb)
        ap = self.t.ap[:, self.off // 4:(self.off + nb) // 4]
        if dt != F32:
            ap = ap.bitcast(dt)
        ap = ap[:, 0:n]
        if len(free_shape) == 2:
            ap = ap.rearrange("p (a b) -> p a b", a=free_shape[0])
        elif len(free_shape) == 3:
            ap = ap.rearrange("p (a b c) -> p a b c", a=free_shape[0], b=free_shape[1])
        self.off += nb
        self.n += 1
        return T(ap, "%s#%d" % (name, self.n))


BG = []


def bg_emit(n):
    for _ in range(min(n, len(BG))):
        BG.pop(0)()


def bg_precast(P, wgu, wd, wgu_bf, wd_bf):
    for j in range(NFC):
        for t in range(2):
            BG.append(lambda j=j, t=t: P.dma("pool", T(wgu_bf.ap[j, t], (wgu_bf.key, j, t)), T(wgu.ap[j, t], wgu.key)))
    for c in range(NDC):
        BG.append(lambda c=c: P.dma("pool", T(wd_bf.ap[c], (wd_bf.key, c)), T(wd.ap[c], wd.key)))


def down_ln(P, A, PS, CONST, rhsT, NK, wd, lcol, z, xres_in, xT_out, xres_out, out_tok, t0, bufs, wd_bf=None, first=0, stagger=False):
    ones = CONST["ones"]
    g_col, b_col = lcol
    wds, sq, mu, rstd, tmp, ob, otok = bufs
    P.dma("sp", z, T(xres_in.ap[:, t0:t0 + TB].rearrange("(c p) t -> p c t", p=128), xres_in.key),
          wkeys=[z.k(c) for c in range(NDC)])
    S1 = PS[6]
    S2 = PS[7]

    def stats(c):
        P.mm(S1, ones, z[:, c, :].k(c), start=(c == 0), stop=(c == NDC - 1))
        q = sq[c % 2]
        P.act(q, z[:, c, :].k(c), AF.Square)
        P.mm(S2, ones, q, start=(c == 0), stop=(c == NDC - 1))

    for c in range(NDC):
        slot = wds[down_ln.di % len(wds)]
        down_ln.di += 1
        sflat = T(slot.ap[:, 0:NK, :].rearrange("p j m -> p (j m)"), slot.key)
        fill_blk = (c % 2) if stagger else 0
        if wd_bf is None or (0 <= first <= fill_blk):
            P.dma("pool", sflat, T(wd.ap[c], wd.key))
            if wd_bf is not None and first == fill_blk:
                P.dma("sp", T(wd_bf.ap[c], (wd_bf.key, c)), sflat)
        else:
            P.dma("sp", sflat, T(wd_bf.ap[c], (wd_bf.key, c)))
        py = PS[4 + (c % 2)]
        for j in range(NK):
            P.mm(py, slot[:, j, :], rhsT[:, j, :], start=(j == 0), stop=(j == NK - 1))
        zc = z[:, c, :].k(c)
        P.stt("dve", zc, zc, ALPHA, py, ALU.mult, ALU.add)
        if c >= 1:
            stats(c - 1)
    stats(NDC - 1)
    P.act(mu, S1, AF.Copy, scale=1.0 / D)
    P.tt("dve", tmp[0], mu, mu, ALU.mult)
    P.stt("dve", tmp[1], S2, 1.0 / D, tmp[0], ALU.mult, ALU.subtract)
    P.ts("dve", tmp[0], tmp[1], LN_EPS, None, ALU.add)
    P.act(tmp[1], tmp[0], AF.Sqrt)
    P.recip(rstd, tmp[1])
    for c in range(NDC):
        zc = z[:, c, :].k(c)
        tt_ = tmp[c % 2]
        P.tt("dve", tt_, zc, mu, ALU.subtract)
        P.tt("dve", tt_, tt_, rstd, ALU.mult)
        P.ts("dve", zc, tt_, g_col[:, c:c + 1], b_col[:, c:c + 1], ALU.mult, ALU.add)
        if out_tok is None:
            P.copy("act", ob[:, c, :].k(c), zc)
    if out_tok is None:
        P.dma("sp", T(xres_out.ap[:, t0:t0 + TB].rearrange("(c p) t -> p c t", p=128), xres_out.key), z,
              rkeys=[z.k(c) for c in range(NDC)])
        P.dma("sp", T(xT_out.ap[:, t0:t0 + TB].rearrange("(c p) t -> p c t", p=128), xT_out.key), ob,
              rkeys=[ob.k(c) for c in range(NDC)])
    else:
        ident_f = CONST["ident_f"]
        for tt in range(TB // 128):
            for c4 in range(NDC // 4):
                pt = PS[c4 % 4]
                for cc in range(4):
                    c = c4 * 4 + cc
                    P.mm(pt[:, cc * 128:(cc + 1) * 128], z[:, c, tt * 128:(tt + 1) * 128].k(c), ident_f)
                P.copy("act" if c4 % 2 else "dve", otok[:, c4 * 512:(c4 + 1) * 512].k(c4), pt)
            P.dma("sp", T(out_tok.ap[t0 + tt * 128:t0 + (tt + 1) * 128, :], out_tok.key), otok,
                  rkeys=[otok.k(i) for i in range(4)])


down_ln.di = 0


def ln_bufs(A, NKmax):
    wds = [A.alloc("wd%d" % i, [NKmax, 128], BF16) for i in range(3)]
    sq = [A.alloc("sq%d" % i, [TB], F32) for i in range(2)]
    mu = A.alloc("mu", [TB], F32)
    rstd = A.alloc("rstd", [TB], F32)
    tmp = [A.alloc("tmp%d" % i, [TB], F32) for i in range(2)]
    return wds, sq, mu, rstd, tmp


def ffn_phase(P, A, PS, CONST, xT_in, xres_in, wgu, wd, lcol, xT_out, xres_out, NT, out_tok=None, wgu_bf=None, wd_bf=None,
              cached=False, next_cast=None):
    P.fence()
    A.reset()
    xb = A.alloc("xb", [NDC, TB], BF16)
    gT = A.alloc("gT", [NFC, TB], BF16)
    z = A.alloc("z", [NDC, TB], F32)
    NWG = 4
    wgs = [A.alloc("wg%d" % i, [2, NDC, 128], BF16) for i in range(NWG)]
    sg = [A.alloc("sg%d" % i, [TB], F32) for i in range(2)]
    wds, sq, mu, rstd, tmp = ln_bufs(A, NFC)
    if out_tok is None:
        ob = A.alloc("ob", [NDC, TB], BF16)
        otok = None
    else:
        ob = None
        otok = A.alloc("otok", [D], F32)
    bufs = (wds, sq, mu, rstd, tmp, ob, otok)
    wi = 0
    if next_cast is not None:
        bg_precast(P, *next_cast)
    nblk_total = NT // TB
    for blk in range(NT // TB):
        t0 = blk * TB
        P.dma("sp", xb, T(xT_in.ap[:, t0:t0 + TB].rearrange("(c p) t -> p c t", p=128), xT_in.key))
        for j in range(NFC):
            if blk >= 1:
                bg_emit(1)
            slot = wgs[wi % NWG]
            wi += 1
            for t in range(2):
                sflat = T(slot.ap[:, t].rearrange("p c f -> p (c f)"), (slot.key, t))
                fill_blk = 0
                if wgu_bf is None or (blk <= fill_blk and not cached):
                    P.dma("pool", sflat, T(wgu.ap[j, t], wgu.key))
                    if wgu_bf is not None and blk == fill_blk:
                        P.dma("sp", T(wgu_bf.ap[j, t], (wgu_bf.key, j, t)), sflat)
                else:
                    P.dma("sp", sflat, T(wgu_bf.ap[j, t], (wgu_bf.key, j, t)))
            pa = PS[(j % 2) * 2]
            pu = PS[(j % 2) * 2 + 1]
            for t, ps in ((0, pa), (1, pu)):
                for c in range(NDC):
                    P.mm(ps, slot[:, t, c, :].k(t), xb[:, c, :], start=(c == 0), stop=(c == NDC - 1))
            s_ = sg[j % 2]
            P.act(s_, pa, AF.Silu)
            P.stt("dve", gT[:, j, :], s_, 0.5, pu, ALU.mult, ALU.mult)
        down_ln(P, A, PS, CONST, gT, NFC, wd, lcol, z, xres_in, xT_out, xres_out, out_tok, t0, bufs,
                wd_bf=wd_bf, first=(-1 if cached else blk), stagger=False)
    bg_emit(10 ** 6)


def outproj_phase(P, A, PS, CONST, mT_in, xres_in, wo, lcol, xT_out, xres_out, NT):
    P.fence()
    A.reset()
    mb = [A.alloc("mb%d" % i, [NDC, TB], BF16) for i in range(2)]
    z = A.alloc("z", [NDC, TB], F32)
    ob = A.alloc("ob", [NDC, TB], BF16)
    wds, sq, mu, rstd, tmp = ln_bufs(A, NDC)
    bufs = (wds, sq, mu, rstd, tmp, ob, None)
    for blk in range(NT // TB):
        t0 = blk * TB
        m = mb[blk % 2]
        P.dma("sp", m, T(mT_in.ap[:, t0:t0 + TB].rearrange("(c p) t -> p c t", p=128), mT_in.key))
        down_ln(P, A, PS, CONST, m, NDC, wo, lcol, z, xres_in, xT_out, xres_out, None, t0, bufs)


def prologue_phase(P, A, PS, CONST, x_tok, xT_out, xres_out, NT):
    P.fence()
    A.reset()
    ident_f = CONST["ident_f"]
    xt = [A.alloc("xt%d" % i, [D], F32) for i in range(4)]
    z = A.alloc("z", [NDC, TB], F32)
    ob = A.alloc("ob", [NDC, TB], BF16)
    for blk in range(NT // TB):
        t0 = blk * TB
        for tt in range(4):
            P.dma("sp", xt[tt], T(x_tok.ap[t0 + tt * 128:t0 + (tt + 1) * 128, :], None))
        for c in range(NDC):
            pt = PS[c % 4]
            for tt in range(4):
                P.mm(pt[:, tt * 128:(tt + 1) * 128], xt[tt][:, c * 128:(c + 1) * 128], ident_f)
            P.copy("dve", z[:, c, :].k(c), pt)
            P.copy("act", ob[:, c, :].k(c), z[:, c, :].k(c))
        P.dma("sp", T(xres_out.ap[:, t0:t0 + TB].rearrange("(c p) t -> p c t", p=128), xres_out.key), z,
              rkeys=[z.k(c) for c in range(NDC)])
        P.dma("sp", T(xT_out.ap[:, t0:t0 + TB].rearrange("(c p) t -> p c t", p=128), xT_out.key), ob,
              rkeys=[ob.k(c) for c in range(NDC)])


HN_EPS = 1e-6
BG_PER_SLOT = 3
DBG = 0
DBG2 = 0
NRET = 2
KTM_ENG = "dve"


def bfview(ps):
    return T(ps.ap.bitcast(BF16), ps.key)


def load_w(P, slot, wm, col0, ncols, key=None):
    src = wm.ap[:, col0:col0 + ncols].rearrange("(c p) f -> p c f", p=128)
    P.dma("pool", slot if key is None else slot.k(key), T(src, wm.key))


def proj_fm(P, ps, w, xb, key=None):
    for c in range(NDC):
        wc = w[:, c, :] if key is None else w[:, c, :].k(key)
        P.mm(ps, wc, xb[:, c, :], start=(c == 0), stop=(c == NDC - 1))


def proj_tm(P, ps, w, xb, tt, key=None):
    for c in range(NDC):
        wc = w[:, c, :] if key is None else w[:, c, :].k(key)
        P.mm(ps, xb[:, c, tt * 128:(tt + 1) * 128], wc, start=(c == 0), stop=(c == NDC - 1))


def rstd_col(P, out, var, eps, tmp):
    P.act(tmp, var, AF.Ln, bias=eps)
    P.act(out, tmp, AF.Exp, scale=-0.5)


def head_norm_ln(P, A_, y_ps, ncol, st, mv, tmpc, rs):
    P.op("dve", lambda e: e.bn_stats(st.ap, y_ps.ap), [y_ps], [st])
    P.op("dve", lambda e: e.bn_aggr(mv.ap, st.ap), [st], [mv])
    rstd_col(P, rs, mv[:, 1:2], HN_EPS, tmpc)
    return mv[:, 0:1], rs


def ret_stage(P, A, PS, K, xblk, wm, col0, tabs, gn_bc, out_rows, S, gC):
    P.fence()
    A.reset()
    nblk = S // TB
    ws = [A.alloc("rw%d" % i, [NDC, 128], BF16) for i in range(4)]
    wvg = A.alloc("rwvg", [NDC, 256], BF16)
    for i in range(4):
        load_w(P, ws[i], wm, col0 + i * 128, 128)
    load_w(P, wvg, wm, col0 + 512, 256)
    xb = [A.alloc("xb%d" % i, [NDC, TB], BF16) for i in range(2)]
    tb4 = [A.alloc("tab%d" % i, [4, TB], F32) for i in range(2)]
    QT = [A.alloc("QT%d" % i, [TB], BF16) for i in range(2)]
    KT = [A.alloc("KT%d" % i, [TB], BF16) for i in range(2)]
    V = [A.alloc("V%d" % i, [4, 128], BF16) for i in range(2)]
    G = [A.alloc("G%d" % i, [4, 128], F32) for i in range(2)]
    Ktm = [A.alloc("Ktm%d" % i, [4, 128], BF16) for i in range(2)]
    outT = [A.alloc("outT%d" % i, [TB], BF16) for i in range(2)]
    R32 = A.alloc("R32", [128], F32)
    Rbf = A.alloc("Rbf", [128], BF16)
    tmpR = A.alloc("tmpR", [128], F32)
    t1 = [A.alloc("t1_%d" % i, [TB], F32) for i in range(2)]
    t2 = [A.alloc("t2_%d" % i, [TB], F32) for i in range(2)]
    ATm = [A.alloc("ATm%d" % i, [128], BF16) for i in range(2)]
    yn = [A.alloc("yn%d" % i, [128], F32) for i in range(2)]
    ob = [A.alloc("yo%d" % i, [128], BF16) for i in range(2)]
    st = [A.alloc("st%d" % i, [6], F32) for i in range(2)]
    mv = [A.alloc("mv%d" % i, [2], F32) for i in range(2)]
    tc_ = [A.alloc("tc%d" % i, [1], F32) for i in range(2)]
    rs = [A.alloc("rs%d" % i, [1], F32) for i in range(2)]
    P.memset("dve", R32, 0.0)
    P.memset("dve", Rbf, 0.0)
    maskT = K["maskT_f"]
    ident = K["ident_b"]
    ci = 0
    for tb in range(nblk):
        t0 = tb * TB
        x = xb[tb % 2]
        xblk(x, tb)
        tab = tb4[tb % 2]
        P.dma("sp", tab, T(tabs.ap[:, :, t0:t0 + TB].rearrange("f p t -> p f t"), None))
        q, k = QT[tb % 2], KT[tb % 2]
        for which, dst in ((0, q), (1, k)):
            pa, pb = PS[which * 2], PS[which * 2 + 1]
            proj_fm(P, pa, ws[which * 2], x)
            proj_fm(P, pb, ws[which * 2 + 1], x)
            a1, a2 = t1[which], t2[which]
            P.tt("dve", a1, pa, tab[:, which * 2, :], ALU.mult)
            P.tt("dve", a2, pb, tab[:, which * 2 + 1, :], ALU.mult)
            P.tt("dve", dst, a1, a2, ALU.add)
        v, g, ktm = V[tb % 2], G[tb % 2], Ktm[tb % 2]
        if DBG == 1:
            P.dma("sp", T(out_rows.ap[:, t0:t0 + TB], out_rows.key), q)
            continue
        for tt in range(4):
            if DBG2 & 1:
                break
            pv = PS[4]
            proj_tm(P, pv[:, 0:256], wvg, x, tt)
            if DBG2 & 2:
                continue
            P.copy("dve", v[:, tt, :].k(tt), pv[:, 0:128])
            if DBG2 & 4:
                continue
            P.act(g[:, tt, :].k(tt), pv[:, 128:256], AF.Silu)
        pT = PS[5]
        for tt in range(4):
            if DBG2 & 8:
                break
            pTt = pT[:, tt * 128:(tt + 1) * 128]
            P.mm(pTt, (q if DBG2 & 64 else k)[:, tt * 128:(tt + 1) * 128], ident)
            if DBG2 & 16:
                continue
            P.copy(KTM_ENG, ktm[:, tt, :].k(tt), pTt)
        oT = outT[tb % 2]
        if DBG == 2:
            P.dma("sp", T(out_rows.ap[:, t0:t0 + TB], out_rows.key), q, rkeys=[q] + [ktm[:, 0, :].k(i) for i in range(4)])
            continue
        for tt in range(4):
            sl = slice(tt * 128, (tt + 1) * 128)
            pAT = PS[6][:, 0:128]
            py = PS[7][:, 0:128]
            pU = PS[5][:, 0:128]
            P.mm(pAT, k[:, sl], q[:, sl])
            am = ATm[ci % 2]
            P.tt("dve", am, pAT, maskT, ALU.mult)
            P.mm(py, am, v[:, tt, :].k(tt), start=True, stop=False)
            P.mm(py, q[:, sl], Rbf, start=False, stop=True)
            P.mm(pU, ktm[:, tt, :].k(tt), v[:, tt, :].k(tt))
            P.tt("dve", tmpR, pU, R32, ALU.add)
            P.ts("dve", R32, tmpR, gC, None, ALU.mult)
            P.ts("dve", Rbf, tmpR, gC, None, ALU.mult)
            i2 = ci % 2
            mean, rstd = head_norm_ln(P, A, py, 128, st[i2], mv[i2], tc_[i2], rs[i2])
            P.ts("dve", yn[i2], py, mean, rstd, ALU.subtract, ALU.mult)
            P.tt("dve", yn[i2], yn[i2], gn_bc, ALU.mult)
            P.tt("dve", ob[i2], yn[i2], g[:, tt, :].k(tt), ALU.mult)
            po = PS[4][:, 256:384]
            P.mm(po, ob[i2], ident)
            P.copy("dve", oT[:, sl].k(tt), po)
            ci += 1
        P.dma("sp", T(out_rows.ap[:, t0:t0 + TB], out_rows.key), oT, rkeys=[oT[:, 0:1].k(tt) for tt in range(4)])


def diff_stage(P, A, PS, K, xblk, wm, col0, tabs, gd_bc, neglam, out_rows, S):
    P.fence()
    A.reset()
    nblk = S // TB
    NT_ = S // 128
    ws = [A.alloc("dw%d" % i, [NDC, 128], BF16) for i in range(8)]
    wv = A.alloc("dwv", [NDC, 256], BF16)
    for i in range(0, 8, 2):
        load_w(P, ws[i], wm, col0 + i * 128, 128)
    load_w(P, wv, wm, col0 + 1024, 256)
    xb = [A.alloc("xb%d" % i, [NDC, TB], BF16) for i in range(2)]
    tb2 = [A.alloc("tab%d" % i, [2, TB], F32) for i in range(2)]
    QK = [A.alloc("QK%d" % i, [S], BF16) for i in range(4)]
    Va = A.alloc("Va", [NT_, 257], BF16)
    t1 = [A.alloc("t1_%d" % i, [TB], F32) for i in range(2)]
    t2 = [A.alloc("t2_%d" % i, [TB], F32) for i in range(2)]
    E = [A.alloc("E%d" % i, [TB], BF16) for i in range(6)]
    o1 = A.alloc("o1", [4, 256], F32)
    o2 = A.alloc("o2", [4, 256], F32)
    sqj = A.alloc("sqj", [256], F32)
    obf = [A.alloc("obf%d" % i, [256], BF16) for i in range(2)]
    outT = [A.alloc("outT%d" % i, [2, TB], BF16) for i in range(2)]
    cols = [A.alloc("cols%d" % i, [4], F32) for i in range(8)]
    ident = K["ident_b"]
    dmask = K["dmask"]
    P.op("dve", lambda e: e.memset(Va.ap[:, :, 256:257], 1.0), [], [Va.k(i) for i in range(NT_)])
    for tb in range(nblk):
        t0 = tb * TB
        x = xb[tb % 2]
        xblk(x, tb)
        tab = tb2[tb % 2]
        P.dma("sp", tab, T(tabs.ap[:, :, t0:t0 + TB].rearrange("f p t -> p f t"), None))
        bg_emit(BG_PER_SLOT)
        for w in range(4):
            pa, pb = PS[(w % 2) * 2], PS[(w % 2) * 2 + 1]
            proj_fm(P, pa, ws[w * 2], x)
            a1, a2 = t1[w % 2], t2[w % 2]
            P.act(a1, pa, AF.Copy)
            P.mm(pb, K["pm_d"], a1)
            P.tt("dve", a2, pb, tab[:, 1, :], ALU.mult)
            P.tt("dve", a1, a1, tab[:, 0, :], ALU.mult)
            P.tt("dve", QK[w][:, t0:t0 + TB].k(tb), a1, a2, ALU.add)
        for tt in range(4):
            pv = PS[4 + tt % 2]
            proj_tm(P, pv[:, 0:256], wv, x, tt)
            P.copy("act", Va[:, tb * 4 + tt, 0:256].k(tb * 4 + tt), pv[:, 0:256])
    scale = 128.0 ** -0.5
    ei = 0
    for qt in range(nblk):
        q0 = qt * TB
        bg_emit(BG_PER_SLOT)
        for m in range(2):
            qT = QK[m][:, q0:q0 + TB].k(qt)
            acc = [PS[2 + i] for i in range(4)]
            nkb = 4 * qt + 4
            SB = (PS[0], PS[1], PS[7])
            DPT = 2
            ebuf = {}

            def score(kb):
                pS = SB[kb % 3]
                P.mm(pS, QK[2 + m][:, kb * 128:(kb + 1) * 128].k(kb // 4), qT)
                e = E[kb % 4]
                ebuf[kb] = e
                P.act(e, pS, AF.Exp, scale=scale)
                r = kb - 4 * qt
                if r >= 0:
                    P.tt("dve", e, e, dmask[:, 384 - 128 * r: 384 - 128 * r + TB], ALU.mult)

            def pv(kb):
                e = ebuf.pop(kb)
                r = kb - 4 * qt
                for qi in range(4):
                    if r > qi:
                        continue
                    last = (kb == min(nkb - 1, 4 * qt + qi))
                    P.mm(acc[qi][:, 0:257], e[:, qi * 128:(qi + 1) * 128], Va[:, kb, :].k(kb),
                         start=(kb == 0), stop=last)

            for i in range(nkb + DPT):
                if i < nkb:
                    score(i)
                if i >= DPT:
                    pv(i - DPT)
            c = cols[m]
            for qi in range(4):
                P.recip(c[:, qi:qi + 1].k(qi), acc[qi][:, 256:257])
                if m == 0:
                    P.ts("dve", o1[:, qi, :].k(qi), acc[qi][:, 0:256], c[:, qi:qi + 1].k(qi), None, ALU.mult)
                else:
                    P.ts("dve", o2[:, qi, :].k(qi), acc[qi][:, 0:256], c[:, qi:qi + 1].k(qi), None, ALU.mult)
        oT = outT[qt % 2]
        for qi in range(4):
            o = o1[:, qi, :].k(qi)
            P.stt("dve", o, o2[:, qi, :].k(qi), neglam, o, ALU.mult, ALU.add)
            ss = cols[2][:, qi:qi + 1].k(qi)
            P.tt("dve", sqj, o, o, ALU.mult)
            P.rsum(ss, sqj)
            tcol = cols[3][:, qi:qi + 1].k(qi)
            rcol = cols[4][:, qi:qi + 1].k(qi)
            P.ts("dve", tcol, ss, 1.0 / 256, HN_EPS, ALU.mult, ALU.add)
            P.act(tcol, tcol, AF.Ln)
            P.act(rcol, tcol, AF.Exp, scale=-0.5)
            ob_ = obf[qi % 2]
            P.stt("dve", ob_, o, rcol, gd_bc, ALU.mult, ALU.mult)
            for hh in range(2):
                ri = qi * 2 + hh
                po = PS[6][:, (ri % 4) * 128:(ri % 4 + 1) * 128]
                P.mm(po, ob_[:, hh * 128:(hh + 1) * 128], ident)
                P.copy("dve", oT[:, hh, qi * 128:(qi + 1) * 128].k(qi * 2 + hh), po)
        P.dma("sp", T(out_rows.ap[:, q0:q0 + TB].rearrange("(h p) t -> p h t", p=128), out_rows.key), oT,
              rkeys=[oT[:, 0, 0:1].k(i) for i in range(8)])


def mlstm_stage(P, A, PS, K, xblk, wm, col0, cw, gb, gm_bc, out_rows, S):
    P.fence()
    A.reset()
    nblk = S // TB
    NT_ = S // 128
    R = [A.alloc("R%d" % i, [S + 8], F32) for i in range(4)]
    qT = A.alloc("qT", [S], BF16)
    kT = A.alloc("kT", [S], BF16)
    qpT = A.alloc("qpT", [S], BF16)
    Va = A.alloc("Va", [NT_, 129], BF16)
    Og = A.alloc("Og", [NT_, 128], BF16)
    mark0 = A.off
    wq = A.alloc("mwq", [NDC, 128], BF16)
    wk = A.alloc("mwk", [NDC, 128], BF16)
    wi = A.alloc("mwi", [NDC, 128], BF16)
    wf = A.alloc("mwf", [NDC, 128], BF16)
    wvo = A.alloc("mwvo", [NDC, 256], BF16)
    for i, sl in enumerate((wq, wk, wi, wf)):
        load_w(P, sl, wm, col0 + i * 128, 128)
    load_w(P, wvo, wm, col0 + 512, 256)
    xb = [A.alloc("xb%d" % i, [NDC, TB], BF16) for i in range(2)]
    ident = K["ident_b"]
    ident_f = K["ident_f"]
    maskT = K["maskT_f"]
    qraw, kraw, Ipre, Fpre = R
    P.memset("dve", qraw[:, 0:8], 0.0)
    P.memset("dve", kraw[:, 0:8], 0.0)
    P.op("dve", lambda e: e.memset(Va.ap[:, :, 128:129], 1.0), [], [Va.k(i) for i in range(NT_)])
    for tb in range(nblk):
        t0 = tb * TB
        x = xb[tb % 2]
        xblk(x, tb)
        bg_emit(BG_PER_SLOT)
        for i, (w, dst, off) in enumerate(((wq, qraw, 3), (wk, kraw, 3), (wi, Ipre, 0), (wf, Fpre, 0))):
            ps = PS[i]
            proj_fm(P, ps, w, x)
            P.copy("act" if i % 2 else "dve", dst[:, off + t0: off + t0 + TB], ps)
        for tt in range(4):
            pv = PS[4 + tt % 2]
            proj_tm(P, pv[:, 0:256], wvo, x, tt)
            P.copy("dve", Va[:, tb * 4 + tt, 0:128].k(tb * 4 + tt), pv[:, 0:128])
            P.act(Og[:, tb * 4 + tt, :].k(tb * 4 + tt), pv[:, 128:256], AF.Sigmoid)
    P.fence()
    A.off = mark0
    NB = A.alloc("NB", [S + 8], F32)
    Mx = A.alloc("Mx", [S + 8], F32)
    onesr = A.alloc("onesr", [TB], F32)
    colG = A.alloc("colG", [NT_], F32)
    colEW = A.alloc("colEW", [NT_], F32)
    colENM = A.alloc("colENM", [NT_], F32)
    Mc = A.alloc("Mc", [NT_ + 1], F32)
    NMc = A.alloc("NMc", [NT_ + 1], F32)
    ngb = A.alloc("ngb", [2], F32)
    DEC = A.alloc("DEC", [NT_], F32)
    junk = A.alloc("junk", [NT_, 128], F32)
    S32 = A.alloc("S32", [129], F32)
    Sbf = A.alloc("Sbf", [129], BF16)
    WT = [A.alloc("WT%d" % i, [128], F32) for i in range(2)]
    WTm = [A.alloc("WTm%d" % i, [128], F32) for i in range(2)]
    scT = [A.alloc("scT%d" % i, [128], BF16) for i in range(2)]
    KW = [A.alloc("KW%d" % i, [128], BF16) for i in range(2)]
    hh = [A.alloc("hh%d" % i, [128], F32) for i in range(2)]
    hb = [A.alloc("hb%d" % i, [128], BF16) for i in range(2)]
    st = [A.alloc("st%d" % i, [6], F32) for i in range(2)]
    mv = [A.alloc("mv%d" % i, [2], F32) for i in range(2)]
    tc_ = [A.alloc("tc%d" % i, [1], F32) for i in range(2)]
    rs = [A.alloc("rs%d" % i, [1], F32) for i in range(2)]
    dcol = [A.alloc("dcol%d" % i, [1], F32) for i in range(2)]
    outT = [A.alloc("outT%d" % i, [TB], BF16) for i in range(2)]
    P.memset("dve", onesr, 1.0)
    P.memset("dve", S32, 0.0)
    P.memset("dve", Sbf, 0.0)
    P.ts("dve", ngb, gb, -1.0, None, ALU.mult)
    for (raw, dst, wofs, bofs, scl) in ((qraw, qT, 0, 8, None), (kraw, kT, 4, 9, 128.0 ** -0.5)):
        for hb_ in range(0, S, 2048):
            n = min(2048, S - hb_)
            acc = NB[:, hb_:hb_ + n]
            P.ts("dve", acc, raw[:, hb_:hb_ + n], cw[:, wofs:wofs + 1], cw[:, bofs:bofs + 1], ALU.mult, ALU.add)
            for j in range(1, 4):
                P.stt("dve", acc, raw[:, hb_ + j:hb_ + j + n], cw[:, wofs + j:wofs + j + 1], acc, ALU.mult, ALU.add)
            if scl is None:
                P.act(dst[:, hb_:hb_ + n], acc, AF.Silu)
            else:
                P.act(acc, acc, AF.Silu)
                P.act(dst[:, hb_:hb_ + n], acc, AF.Copy, scale=scl)
    Lr = Fpre
    P.act(Lr[:, 0:S], Fpre[:, 0:S], AF.Exp, scale=-1.0, bias=ngb[:, 1:2])
    P.act(Lr[:, 0:S], Lr[:, 0:S], AF.Ln, bias=1.0)
    for tb in range(nblk):
        t0 = tb * TB
        init = 0.0 if tb == 0 else NB[:, t0 - 1:t0]
        P.scan(NB[:, t0:t0 + TB], onesr, Lr[:, t0:t0 + TB], init, ALU.mult, ALU.add)
    G = Ipre
    P.stt("dve", G[:, 0:S], Ipre[:, 0:S], gb[:, 0:1], NB[:, 0:S], ALU.add, ALU.add)
    for tb in range(nblk):
        t0 = tb * TB
        init = 0.0 if tb == 0 else Mx[:, t0 - 1:t0]
        P.scan(Mx[:, t0:t0 + TB], G[:, t0:t0 + TB], G[:, t0:t0 + TB], init, ALU.max, ALU.max)
    P.memset("dve", Mc[:, 0:1], 0.0)
    P.copy("dve", Mc[:, 1:NT_ + 1], T(Mx.ap[:, 0:S].rearrange("p (n c) -> p n c", c=128)[:, :, 127], Mx.key))
    P.ts("dve", NMc, Mc, -1.0, None, ALU.mult)
    P.tt("dve", DEC, Mc[:, 0:NT_], Mc[:, 1:NT_ + 1], ALU.subtract)
    P.act(DEC, DEC, AF.Exp)
    Mx3 = T(Mx.ap[:, 0:S].rearrange("p (n c) -> p n c", c=128), Mx.key)
    G3 = T(G.ap[:, 0:S].rearrange("p (n c) -> p n c", c=128), G.key)
    NB3 = T(NB.ap[:, 0:S].rearrange("p (n c) -> p n c", c=128), NB.key)
    EW = qraw
    ENM = kraw
    EIN = Fpre
    EW3 = T(EW.ap[:, 0:S].rearrange("p (n c) -> p n c", c=128), EW.key)
    ENM3 = T(ENM.ap[:, 0:S].rearrange("p (n c) -> p n c", c=128), ENM.key)
    EIN3 = T(EIN.ap[:, 0:S].rearrange("p (n c) -> p n c", c=128), EIN.key)
    bc = lambda t_, lo: T(t_.ap[:, lo:lo + NT_].unsqueeze(2).to_broadcast([128, NT_, 128]), t_.key)
    P.tt("dve", EW3, G3, bc(NMc, 1), ALU.add)
    P.act(EW[:, 0:S], EW[:, 0:S], AF.Exp)
    P.tt("dve", EIN3, bc(Mc, 0), Mx3, ALU.subtract)
    P.act(EIN[:, 0:S], EIN[:, 0:S], AF.Exp)
    P.tt("dve", ENM[:, 0:S], NB[:, 0:S], Mx[:, 0:S], ALU.subtract)
    P.act(ENM[:, 0:S], ENM[:, 0:S], AF.Exp)
    P.tt("dve", qpT, qT, EIN[:, 0:S], ALU.mult)
    ident_bc = T(ident_f.ap.unsqueeze(1).to_broadcast([128, NT_, 128]), ident_f.key)
    for (src3, col) in ((G3, colG), (EW3, colEW), (ENM3, colENM)):
        P.tt("dve", junk, src3, ident_bc, ALU.mult)
        P.rsum(col, junk)
    def front(n):
        sl = slice(n * 128, (n + 1) * 128)
        i2 = n % 2
        pS = PS[n % 2][:, 0:128]
        pN = PS[2 + n % 2][:, 0:129]
        pU = PS[4 + n % 2][:, 0:129]
        P.ts("dve", WT[i2], Mx[:, sl], colG[:, n:n + 1], 0.0, ALU.subtract, ALU.max)
        P.mm(pS, kT[:, sl], qT[:, sl])
        yield
        P.act(WT[i2], WT[i2], AF.Exp, scale=-1.0)
        pk = PS[6][:, 0:128]
        P.mm(pk, kT[:, sl], ident)
        yield
        P.tt("dve", WTm[i2], WT[i2], maskT, ALU.mult)
        P.ts("dve", KW[i2], pk, colEW[:, n:n + 1], None, ALU.mult)
        yield
        P.tt("dve", scT[i2], pS, WTm[i2], ALU.mult)
        yield
        P.mm(pN, scT[i2], Va[:, n, :].k(n), start=True, stop=False)
        P.mm(pN, qpT[:, sl], Sbf, start=False, stop=True)
        P.mm(pU, KW[i2], Va[:, n, :].k(n))
        yield
        P.stt("dve", S32, S32, DEC[:, n:n + 1], pU, ALU.mult, ALU.add)
        yield
        P.copy("dve", Sbf, S32)

    def back(n):
        i2 = n % 2
        pN = PS[2 + n % 2][:, 0:129]
        P.copy("dve", tc_[i2], pN[:, 128:129])
        yield
        P.stt("dve", dcol[i2], tc_[i2], -1.0, tc_[i2], ALU.mult, ALU.max)
        yield
        P.ts("dve", dcol[i2], dcol[i2], colENM[:, n:n + 1], None, ALU.max)
        yield
        P.recip(dcol[i2], dcol[i2])
        yield
        P.stt("dve", hh[i2], pN[:, 0:128], dcol[i2], Og[:, n, :].k(n), ALU.mult, ALU.mult)
        yield
        P.op("dve", lambda e: e.bn_stats(st[i2].ap, hh[i2].ap), [hh[i2]], [st[i2]])
        yield
        P.op("dve", lambda e: e.bn_aggr(mv[i2].ap, st[i2].ap), [st[i2]], [mv[i2]])
        yield
        P.act(tc_[i2], mv[i2][:, 1:2], AF.Ln, bias=HN_EPS)
        yield
        P.act(rs[i2], tc_[i2], AF.Exp, scale=-0.5)
        yield
        P.ts("dve", hh[i2], hh[i2], mv[i2][:, 0:1], rs[i2], ALU.subtract, ALU.mult)
        yield
        P.tt("dve", hb[i2], hh[i2], gm_bc, ALU.mult)
        yield
        po = PS[7][:, 0:128]
        P.mm(po, hb[i2], ident)
        yield
        oT = outT[(n // 4) % 2]
        P.copy("dve", oT[:, (n % 4) * 128:(n % 4 + 1) * 128].k(n % 4), po)
        if n % 4 == 3:
            t0 = (n // 4) * TB
            P.dma("sp", T(out_rows.ap[:, t0:t0 + TB], out_rows.key), oT, rkeys=[oT[:, 0:1].k(i) for i in range(4)])

    for n in range(NT_ + 1):
        gens = []
        if n % 4 == 0:
            bg_emit(BG_PER_SLOT)
        if n < NT_:
            gens.append(front(n))
        if n >= 1:
            gens.append(back(n - 1))
        while gens:
            for gnr in list(gens):
                try:
                    next(gnr)
                except StopIteration:
                    gens.remove(gnr)


def ret_stage2(P, A, PS, K, xblk, wm, col0s, tabs2, gn_bcs, out_rows2, S, gCs):
    P.fence()
    A.reset()
    nblk = S // TB
    H = 2
    ws = [[A.alloc("rw%d_%d" % (h, i), [NDC, 128], BF16) for i in range(4)] for h in range(H)]
    wvg = [A.alloc("rwvg%d" % h, [NDC, 256], BF16) for h in range(H)]
    for h in range(H):
        for i in (0, 2):
            load_w(P, ws[h][i], wm, col0s[h] + i * 128, 128)
        load_w(P, wvg[h], wm, col0s[h] + 512, 256)
    xb = [A.alloc("xb%d" % i, [NDC, TB], BF16) for i in range(2)]
    al2 = lambda nm, shp, dt: [[A.alloc("%s%d_%d" % (nm, h, i), shp, dt) for i in range(2)] for h in range(H)]
    tb4 = al2("tab", [4, TB], F32)
    QT = al2("QT", [TB], BF16)
    KT = al2("KT", [TB], BF16)
    V = al2("V", [4, 128], BF16)
    G = al2("G", [4, 128], F32)
    Ktm = al2("Ktm", [4, 128], BF16)
    outT = al2("outT", [TB], BF16)
    ATm = al2("ATm", [128], BF16)
    yn = al2("yn", [128], F32)
    ob = al2("yo", [128], BF16)
    st = al2("st", [6], F32)
    mv = al2("mv", [2], F32)
    tc_ = al2("tc", [1], F32)
    rs = al2("rs", [1], F32)
    R32 = [A.alloc("R32_%d" % h, [128], F32) for h in range(H)]
    Rbf = [A.alloc("Rbf_%d" % h, [128], BF16) for h in range(H)]
    tmpR = [A.alloc("tmpR_%d" % h, [128], F32) for h in range(H)]
    t1 = [A.alloc("t1_%d" % i, [TB], F32) for i in range(2)]
    t2 = [A.alloc("t2_%d" % i, [TB], F32) for i in range(2)]
    for h in range(H):
        P.memset("dve", R32[h], 0.0)
        P.memset("dve", Rbf[h], 0.0)
    maskT = K["maskT_f"]
    ident = K["ident_b"]
    ci = [0, 0]

    def chunk(h, tb, tt):
        q, k = QT[h][tb % 2], KT[h][tb % 2]
        v, g, ktm = V[h][tb % 2], G[h][tb % 2], Ktm[h][tb % 2]
        oT = outT[h][tb % 2]
        sl = slice(tt * 128, (tt + 1) * 128)
        i2 = ci[h] % 2
        ci[h] += 1
        pAT = PS[4 + 2 * h][:, 0:128]
        pU = PS[4 + 2 * h][:, 128:256]
        py = PS[5 + 2 * h][:, 0:128]
        po = PS[5 + 2 * h][:, 128:256]
        P.mm(pAT, k[:, sl], q[:, sl])
        yield
        am = ATm[h][i2]
        P.tt("dve", am, pAT, maskT, ALU.mult)
        yield
        P.mm(py, am, v[:, tt, :].k(tt), start=True, stop=False)
        P.mm(py, q[:, sl], Rbf[h], start=False, stop=True)
        P.mm(pU, ktm[:, tt, :].k(tt), v[:, tt, :].k(tt))
        yield
        P.op("dve", lambda e: e.bn_stats(st[h][i2].ap, py.ap), [py], [st[h][i2]])
        P.tt("dve", tmpR[h], pU, R32[h], ALU.add)
        yield
        P.op("dve", lambda e: e.bn_aggr(mv[h][i2].ap, st[h][i2].ap), [st[h][i2]], [mv[h][i2]])
        P.ts("dve", R32[h], tmpR[h], gCs[h], None, ALU.mult)
        P.ts("dve", Rbf[h], tmpR[h], gCs[h], None, ALU.mult)
        yield
        P.act(tc_[h][i2], mv[h][i2][:, 1:2], AF.Ln, bias=HN_EPS)
        yield
        P.act(rs[h][i2], tc_[h][i2], AF.Exp, scale=-0.5)
        yield
        P.ts("dve", yn[h][i2], py, mv[h][i2][:, 0:1], rs[h][i2], ALU.subtract, ALU.mult)
        yield
        P.tt("dve", yn[h][i2], yn[h][i2], gn_bcs[h], ALU.mult)
        P.tt("dve", ob[h][i2], yn[h][i2], g[:, tt, :].k(tt), ALU.mult)
        yield
        P.mm(po, ob[h][i2], ident)
        yield
        P.copy("dve", oT[:, sl].k(tt), po)

    for tb in range(nblk):
        t0 = tb * TB
        x = xb[tb % 2]
        xblk(x, tb)
        bg_emit(BG_PER_SLOT)
        for h in range(H):
            tab = tb4[h][tb % 2]
            P.dma("sp", tab, T(tabs2[h].ap[:, :, t0:t0 + TB].rearrange("f p t -> p f t"), None))
            for which, dst in ((0, QT[h][tb % 2]), (1, KT[h][tb % 2])):
                pa, pb = PS[which * 2], PS[which * 2 + 1]
                proj_fm(P, pa, ws[h][which * 2], x)
                a1, a2 = t1[which], t2[which]
                P.act(a1, pa, AF.Copy)
                P.mm(pb, K["pm_r"], a1)
                P.tt("dve", a2, pb, tab[:, which * 2 + 1, :], ALU.mult)
                P.tt("dve", a1, a1, tab[:, which * 2, :], ALU.mult)
                P.tt("dve", dst, a1, a2, ALU.add)
            v, g, ktm = V[h][tb % 2], G[h][tb % 2], Ktm[h][tb % 2]
            k = KT[h][tb % 2]
            for tt in range(4):
                pv = PS[tt % 2]
                proj_tm(P, pv[:, 0:256], wvg[h], x, tt)
                P.copy("dve", v[:, tt, :].k(tt), pv[:, 0:128])
                P.act(g[:, tt, :].k(tt), pv[:, 128:256], AF.Silu)
            for tt in range(4):
                pTt = PS[2 + tt % 2][:, 0:128]
                P.mm(pTt, k[:, tt * 128:(tt + 1) * 128], ident)
                P.copy("dve", ktm[:, tt, :].k(tt), pTt)
        for tt in range(4):
            gens = [chunk(h, tb, tt) for h in range(H)]
            alive = list(gens)
            while alive:
                for gnr in list(alive):
                    try:
                        next(gnr)
                    except StopIteration:
                        alive.remove(gnr)
        for h in range(H):
            oT = outT[h][tb % 2]
            P.dma("sp", T(out_rows2[h].ap[:, t0:t0 + TB], out_rows2[h].key), oT,
                  rkeys=[oT[:, 0:1].k(tt) for tt in range(4)])


NSP = 1306
MIXW = 5632


def mixer_cols(g):
    cols = []
    ar = np.arange(128)
    perm_r = (ar + 64) % 128
    perm_d = ar.copy()
    perm_d[0:16] = ar[0:16] + 16
    perm_d[16:32] = ar[16:32] - 16
    for r in range(2):
        h = 2 * g + r
        q0, k0, v0, g0 = 0 + h * 128, 512 + h * 128, 1024 + h * 128, 1536 + h * 128
        cols += [q0 + ar, q0 + perm_r, k0 + ar, k0 + perm_r, v0 + ar, g0 + ar]
    for r in range(2):
        h = 2 * g + r
        for base in (2048, 3072):
            for m in range(2):
                b0 = base + h * 256 + m * 128
                cols += [b0 + ar, b0 + perm_d]
        cols += [4096 + h * 256 + np.arange(256)]
    for r in range(2):
        h = 2 * g + r
        cols += [5120 + h * 128 + ar, 5632 + h * 128 + ar,
                 np.full(128, 7168 + h), np.full(128, 7172 + h),
                 6144 + h * 128 + ar, 6656 + h * 128 + ar]
    return np.concatenate(cols)


def mixed_order(g):
    o = []
    for r in range(2):
        o.append((2 * g + r) * 128 + np.arange(128))
    for r in range(2):
        o.append(512 + (2 * g + r) * 256 + np.arange(256))
    for r in range(2):
        o.append(1536 + (2 * g + r) * 128 + np.arange(128))
    return np.concatenate(o)


def rope_np(S, rot_dim, theta):
    pos = np.arange(S, dtype=np.float32)
    inv = (np.float32(theta) ** (-np.arange(0, rot_dim, 2, dtype=np.float32) / np.float32(rot_dim))).astype(np.float32)
    ang = (pos[:, None] * inv[None, :]).astype(np.float32)
    return np.cos(ang).astype(np.float32), np.sin(ang).astype(np.float32)


def ret_tables(g, S):
    cos, sin = rope_np(S, 128, 10000.0)
    c = np.concatenate([cos, cos], axis=1).T
    s = np.concatenate([-sin, sin], axis=1).T
    out = np.zeros((2, 4, 128, S), np.float32)
    i = (np.arange(S) % 128).astype(np.float64)
    for r in range(2):
        h = 2 * g + r
        lg = math.log(1.0 - 2.0 ** (-5.0 - h))
        dq = np.exp((i + 1.0) * lg)
        dk = np.exp(-(i + 1.0) * lg) * (128.0 ** -0.5)
        out[r, 0] = c * dq
        out[r, 1] = s * dq
        out[r, 2] = c * dk
        out[r, 3] = s * dk
    return out


def ret_gC(h):
    return float(math.exp(128.0 * math.log(1.0 - 2.0 ** (-5.0 - h))))


def diff_tables(S):
    cos, sin = rope_np(S, 32, 500000.0)
    c = np.ones((128, S), np.float32)
    s = np.zeros((128, S), np.float32)
    c[0:16] = cos.T
    c[16:32] = cos.T
    s[0:16] = -sin.T
    s[16:32] = sin.T
    return np.stack([c, s])


def tile_wgu(w):
    L = w.shape[0]
    return np.ascontiguousarray(w.reshape(L, NDC, 128, 2, NFC, 128).transpose(0, 4, 3, 2, 1, 5)).reshape(L, NFC, 2, 128, NDC * 128)


def tile_wd(w):
    L, Kd = w.shape[0], w.shape[1]
    nk = Kd // 128
    return np.ascontiguousarray(w.reshape(L, nk, 128, NDC, 128).transpose(0, 3, 2, 1, 4)).reshape(L, NDC, 128, nk * 128)


def const_tables():
    ar = np.arange(128)
    maskT = (ar[None, :] >= ar[:, None]).astype(np.float32)
    x = np.arange(896)
    dmask = ((x[None, :] - 384 - ar[:, None]) >= 0).astype(np.float32)
    perm_r = (ar + 64) % 128
    perm_d = ar.copy()
    perm_d[0:16] = ar[0:16] + 16
    perm_d[16:32] = ar[16:32] - 16
    pm_r = (ar[:, None] == perm_r[None, :]).astype(np.float32)
    pm_d = (ar[:, None] == perm_d[None, :]).astype(np.float32)
    return {"ident_f": np.eye(128, dtype=np.float32), "maskT_f": maskT, "dmask": dmask, "pm_r": pm_r, "pm_d": pm_d}


def small_params(g, l, ret_norm_g, diff_lambda, diff_norm_g, conv_w, conv_b, gate_b, mlstm_norm_g):
    sp = np.zeros((128, NSP), np.float32)
    one = np.ones((128, 1), np.float32)
    sp[:, 0:256] = one * ret_norm_g[l][2 * g * 128:(2 * g + 2) * 128][None, :]
    sp[:, 256:512] = one * diff_norm_g[l][None, :]
    sp[:, 512:768] = one * mlstm_norm_g[l][2 * g * 128:(2 * g + 2) * 128][None, :]
    sp[:, 768:1280] = one * diff_lambda[l].reshape(1, 512)
    for r in range(2):
        h = 2 * g + r
        b = 1280 + r * 10
        sp[:, b:b + 4] = conv_w[l][:, h * 128:(h + 1) * 128].T
        sp[:, b + 4:b + 8] = conv_w[l][:, 512 + h * 128:512 + (h + 1) * 128].T
        sp[:, b + 8] = conv_b[l][h * 128:(h + 1) * 128]
        sp[:, b + 9] = conv_b[l][512 + h * 128:512 + (h + 1) * 128]
        sp[:, 1300 + r * 2] = gate_b[l][0, h]
        sp[:, 1300 + r * 2 + 1] = gate_b[l][1, h]
        sp[:, 1304 + r] = ret_gC(h)
    return sp


def mixer_phase(P, A, PS, K, xblk, wm, rtab, dtab, spd, mixedT, S, g, lam_init, sp, lamc):
    P.fence()
    P.dma("sp", sp, spd)
    P.tt("dve", sp[:, 768:896], sp[:, 768:896], sp[:, 896:1024], ALU.mult)
    P.tt("dve", sp[:, 1024:1152], sp[:, 1024:1152], sp[:, 1152:1280], ALU.mult)
    P.rsum(lamc[:, 0:1], sp[:, 768:896])
    P.rsum(lamc[:, 1:2], sp[:, 1024:1152])
    P.act(lamc[:, 2:4], lamc[:, 0:2], AF.Exp)
    P.tt("dve", lamc[:, 4:5], lamc[:, 3:4], lamc[:, 2:3], ALU.subtract)
    P.ts("dve", lamc[:, 5:6], lamc[:, 4:5], -lam_init, None, ALU.add)
    P.ts("dve", sp[:, 256:512], sp[:, 256:512], 1.0 - lam_init, None, ALU.mult)
    neglam = lamc[:, 5:6]
    ret_stage2(P, A, PS, K, xblk, wm, [0, 768], [T(rtab.ap[r], None) for r in range(2)],
               [sp[:, r * 128:(r + 1) * 128] for r in range(2)],
               [T(mixedT.ap[r * 128:(r + 1) * 128, :], mixedT.key) for r in range(2)], S,
               [sp[:, 1304 + r:1305 + r] for r in range(2)])
    for r in range(2):
        diff_stage(P, A, PS, K, xblk, wm, 1536 + r * 1280, dtab, sp[:, 256:512], neglam,
                   T(mixedT.ap[256 + r * 256:256 + (r + 1) * 256, :], mixedT.key), S)
    for r in range(2):
        mlstm_stage(P, A, PS, K, xblk, wm, 4096 + r * 768, sp[:, 1280 + r * 10:1290 + r * 10],
                    sp[:, 1300 + r * 2:1302 + r * 2], sp[:, 512 + r * 128:512 + (r + 1) * 128],
                    T(mixedT.ap[768 + r * 128:768 + (r + 1) * 128, :], mixedT.key), S)


ARENA_BYTES = 196 * 1024
import ml_dtypes
NP_BF16 = ml_dtypes.bfloat16


def _setup_consts(P, cdram, need_mixer):
    K = {}
    K["ones"] = P.sbuf("ones", [128, 128], F32)
    P.memset("dve", K["ones"], 1.0)
    K["ident_f"] = P.sbuf("ident_f", [128, 128], F32)
    P.dma("sp", K["ident_f"], cdram["ident_f"])
    if need_mixer:
        K["maskT_f"] = P.sbuf("maskT_f", [128, 128], F32)
        for nm in ("pm_r", "pm_d"):
            K[nm] = P.sbuf(nm, [128, 128], F32)
            P.dma("sp", K[nm], cdram[nm])
        K["ident_b"] = P.sbuf("ident_b", [128, 128], BF16)
        K["dmask"] = P.sbuf("dmask", [128, 896], BF16)
        P.dma("sp", K["maskT_f"], cdram["maskT_f"])
        P.dma("pool", K["ident_b"], cdram["ident_f"])
        P.dma("pool", K["dmask"], cdram["dmask"])
    return K


def ln_cols(ln_g, ln_b, l, k):
    return np.concatenate([ln_g[l, k].reshape(16, 128).T, ln_b[l, k].reshape(16, 128).T], axis=1).astype(np.float32)


def kernel(**inputs):
    return run_model_fused(inputs, 4, 4096)


PAIRS = [[0, 1], [2, 3], [4, 5], [6, 7]]


def outproj_phase_fused(P, A, PS, CONST, mg_bf, gsel, xres_in, wo, lcol, xT_out, xres_out, NT):
    P.fence()
    A.reset()
    cand = [A.alloc("cand%d" % i, [NDC, TB], BF16) for i in range(2)]
    z = A.alloc("z", [NDC, TB], F32)
    ob = A.alloc("ob", [NDC, TB], BF16)
    wds, sq, mu, rstd, tmp = ln_bufs(A, NDC)
    bufs = (wds, sq, mu, rstd, tmp, ob, None)
    for blk in range(NT // TB):
        t0 = blk * TB
        for h in range(2):
            for r in range(2):
                for k in range(4):
                    src = mg_bf.ap[k, r * 256:(r + 1) * 256, h * NT + t0:h * NT + t0 + TB].rearrange("(c p) t -> p c t", p=128)
                    c0 = r * 8 + k * 2
                    P.dma("sp", cand[h][:, c0:c0 + 2, :].k(c0), T(src, mg_bf.key))
        for c0 in range(0, NDC, 2):
            a = cand[0][:, c0:c0 + 2, :].k(c0)
            b_ = cand[1][:, c0:c0 + 2, :].k(c0)
            P.ts("dve", b_, b_, gsel[:, 1:2], None, ALU.mult)
            P.op("dve", lambda e, a=a.ap, b=b_.ap, s_=gsel.ap[:, 0:1]: e.scalar_tensor_tensor(a, a, s_, b, ALU.mult, ALU.add),
                 [a, b_, gsel], [a, cand[0]])
        down_ln(P, A, PS, CONST, cand[0], NDC, wo, lcol, z, xres_in, xT_out, xres_out, None, t0, bufs)


def build_fused_program(S):
    NT = S // 2
    nbh = NT // TB
    nc = bass.Bass("TRN2", target_bir_lowering=False)
    P = Prog(nc)
    EI = "ExternalInput"
    cdram = {k: P.dram("c_" + k, list(v.shape), F32, kind=EI) for k, v in const_tables().items()}
    K = _setup_consts(P, cdram, True)
    x_tok = P.dram("x_tok", [NT, D], F32, kind=EI)
    wgu1 = P.dram("ffn1_gu", [DEPTH, NFC, 2, 128, NDC * 128], F32, kind=EI)
    wd1 = P.dram("ffn1_d", [DEPTH, NDC, 128, FF], F32, kind=EI)
    wgu2 = P.dram("ffn2_gu", [DEPTH, NFC, 2, 128, NDC * 128], F32, kind=EI)
    wd2 = P.dram("ffn2_d", [DEPTH, NDC, 128, FF], F32, kind=EI)
    wo = P.dram("wo", [DEPTH, NDC, 128, D], F32, kind=EI)
    wm = P.dram("wm", [DEPTH, D, MIXW], F32, kind=EI)
    rtab = P.dram("rtab", [2, 4, 128, S], F32, kind=EI)
    dtab = P.dram("dtab", [2, 128, S], F32, kind=EI)
    spd = P.dram("spd", [DEPTH, 128, NSP], F32, kind=EI)
    lnd = P.dram("lnc", [128, 32 * 3 * DEPTH], F32, kind=EI)
    gseld = P.dram("gsel", [128, 2], F32, kind=EI)
    out_tok = P.dram("out_tok", [NT, D], F32, kind="ExternalOutput")
    lcol = P.sbuf("lcol", [128, 32 * 3 * DEPTH], F32)
    P.dma("sp", lcol, lnd)
    gsel = P.sbuf("gselc", [128, 2], F32)
    P.dma("sp", gsel, gseld)
    sp = P.sbuf("sp", [128, NSP], F32)
    lamc = P.sbuf("lamc", [128, 8], F32)
    lc = lambda l, k: (lcol[:, 32 * (3 * l + k):32 * (3 * l + k) + 16], lcol[:, 32 * (3 * l + k) + 16:32 * (3 * l + k) + 32])
    A = Arena(P, ARENA_BYTES)
    PS = [P.psum("ps%d" % i, [128, 512], F32) for i in range(8)]
    xs32 = P.dram("xs32", [D, NT // 2], F32)
    xg32 = P.dram("xg32", [4, 2 * 512, NT // 2], F32)
    mx32 = P.dram("mx32", [1024, S // 2], F32)
    mg32 = P.dram("mg32", [4, 2 * 256, S // 2], F32)
    xs_bf = T(xs32.ap.bitcast(BF16), xs32.key)
    xg_bf = T(xg32.ap.bitcast(BF16), xg32.key)
    mx_bf = T(mx32.ap.bitcast(BF16), mx32.key)
    mg_bf = T(mg32.ap.bitcast(BF16), mg32.key)
    r_a = P.dram("r_a", [D, NT], F32)
    r_b = P.dram("r_b", [D, NT], F32)
    t_a = P.dram("t_a", [D, NT], BF16)
    t_b = P.dram("t_b", [D, NT], BF16)

    def xblk(x, tb):
        h, t0l = tb // nbh, (tb % nbh) * TB
        for k in range(4):
            src = xg_bf.ap[k, h * 512:(h + 1) * 512, t0l:t0l + TB].rearrange("(c p) t -> p c t", p=128)
            P.dma("sp", x[:, k * 4:(k + 1) * 4, :], T(src, xg_bf.key))

    wcache = [(P.dram("wgu_bf%d" % i, [NFC, 2, 128, NDC * 128], BF16), P.dram("wd_bf%d" % i, [NDC, 128, FF], BF16))
              for i in range(2 * DEPTH)]
    prologue_phase(P, A, PS, K, x_tok, t_a, r_a, NT)
    cur_T, cur_r = t_a, r_a
    for l in range(DEPTH):
        lam_init = 0.8 - 0.6 * math.exp(-0.3 * l)
        ffn_phase(P, A, PS, K, cur_T, cur_r, T(wgu1.ap[l], None), T(wd1.ap[l], None), lc(l, 0), xs_bf, r_b, NT,
                  wgu_bf=wcache[2 * l][0], wd_bf=wcache[2 * l][1], cached=(l > 0),
                  next_cast=(T(wgu2.ap[l], None), T(wd2.ap[l], None), wcache[2 * l + 1][0], wcache[2 * l + 1][1]))
        for k in range(4):
            P.cc("AllGather", PAIRS, T(xg32.ap[k], xg32.key), T(xs32.ap[k * 512:(k + 1) * 512, :], xs32.key))
        mixer_phase(P, A, PS, K, xblk, T(wm.ap[l], None), rtab, dtab, T(spd.ap[l], None), mx_bf, S, 0, lam_init, sp, lamc)
        for k in range(4):
            P.cc("AllGather", PAIRS, T(mg32.ap[k], mg32.key), T(mx32.ap[k * 256:(k + 1) * 256, :], mx32.key))
        outproj_phase_fused(P, A, PS, K, mg_bf, gsel, r_b, T(wo.ap[l], None), lc(l, 1), t_a, r_a, NT)
        if l == DEPTH - 1:
            ffn_phase(P, A, PS, K, t_a, r_a, T(wgu2.ap[l], None), T(wd2.ap[l], None), lc(l, 2), None, None, NT, out_tok=out_tok,
                      wgu_bf=wcache[2 * l + 1][0], wd_bf=wcache[2 * l + 1][1], cached=True)
        else:
            ffn_phase(P, A, PS, K, t_a, r_a, T(wgu2.ap[l], None), T(wd2.ap[l], None), lc(l, 2), t_b, r_b, NT,
                      wgu_bf=wcache[2 * l + 1][0], wd_bf=wcache[2 * l + 1][1], cached=True,
                      next_cast=(T(wgu1.ap[l + 1], None), T(wd1.ap[l + 1], None), wcache[2 * l + 2][0], wcache[2 * l + 2][1]))
            cur_T, cur_r = t_b, r_b
            t_a, t_b = t_b, t_a
            r_a, r_b = r_b, r_a
            cur_T, cur_r = t_a, r_a
    P.emit()
    return nc


def run_model_fused(inp, B, S):
    f32 = lambda a: np.ascontiguousarray(np.asarray(a, dtype=np.float32))
    x = f32(inp["x"])
    NT = S // 2
    ncores = 2 * B
    consts = {"c_" + k: v for k, v in const_tables().items()}
    ln_g, ln_b = f32(inp["ln_g"]), f32(inp["ln_b"])
    w_in, w_out = f32(inp["w_in"]), f32(inp["w_out"])
    depth = w_in.shape[0]
    gorder = np.concatenate([mixed_order(0), mixed_order(1)])
    dtab = diff_tables(S)
    lnc = np.concatenate([ln_cols(ln_g, ln_b, l, k) for l in range(depth) for k in range(3)], axis=1)
    shared = {"ffn1_gu": tile_wgu(f32(inp["ffn1_w_gu"])), "ffn1_d": tile_wd(f32(inp["ffn1_w_down"])),
              "ffn2_gu": tile_wgu(f32(inp["ffn2_w_gu"])), "ffn2_d": tile_wd(f32(inp["ffn2_w_down"])),
              "wo": tile_wd(np.ascontiguousarray(w_out[:, gorder, :])), "dtab": dtab, "lnc": np.ascontiguousarray(lnc)}
    shared.update(consts)
    per_g = []
    for g in range(2):
        sel = np.zeros((128, 2), np.float32)
        sel[:, g] = 1.0
        per_g.append({"wm": np.ascontiguousarray(w_in[:, :, mixer_cols(g)]), "rtab": ret_tables(g, S), "gsel": sel,
                      "spd": np.stack([small_params(g, l, f32(inp["ret_norm_g"]), f32(inp["diff_lambda"]),
                                                    f32(inp["diff_norm_g"]), f32(inp["mlstm_conv_w"]),
                                                    f32(inp["mlstm_conv_b"]), f32(inp["mlstm_gate_b"]),
                                                    f32(inp["mlstm_norm_g"])) for l in range(depth)])})
    maps = []
    for c in range(ncores):
        b, g = c // 2, c % 2
        m = dict(shared)
        m.update(per_g[g])
        m["x_tok"] = np.ascontiguousarray(x[b, g * NT:(g + 1) * NT, :])
        maps.append(m)
    nc = build_fused_program(S)
    res = run_bass_kernel_spmd(nc, maps, core_ids=list(range(ncores)))
    out = np.zeros((B, S, D), np.float32)
    for c in range(ncores):
        b, g = c // 2, c % 2
        out[b, g * NT:(g + 1) * NT, :] = res.results[c]["out_tok"]
    return out
```

```python
import contextlib
import numpy as np
import concourse.bass as bass
import concourse.mybir as mybir
from concourse.bass_utils import run_bass_kernel_spmd

F32 = mybir.dt.float32
BF16 = mybir.dt.bfloat16
AF = mybir.ActivationFunctionType
ALU = mybir.AluOpType
AX = mybir.AxisListType


class T:
    __slots__ = ("ap", "key", "excl")

    def __init__(self, ap, key, excl=False):
        self.ap = ap
        self.key = key
        self.excl = excl

    def __getitem__(self, idx):
        return T(self.ap[idx], self.key, self.excl)

    def k(self, sub):
        return T(self.ap, (self.key, sub), self.excl)


class Op:
    __slots__ = ("eng", "fn", "deps", "sig", "tok", "pos", "dma", "waits", "gidx", "cc")


class Prog:
    ENGS = ("pe", "act", "dve", "pool", "sp")
    RING = 8
    SAMEW = 2

    def __init__(self, nc):
        self.nc = nc
        self.ops = []
        self.lastw = {}
        self.readers = {}
        self.stack = contextlib.ExitStack()
        self.nalloc = 0
        self.last_on = {}
        self.dma_hist = {e: [] for e in self.ENGS}
        self.pending = {}

    def fence(self):
        F = set(self.last_on.values())
        for e in self.ENGS:
            F.update(self.dma_hist[e][-self.RING:])
        self.pending = {e: set(F) for e in self.ENGS}

    def sbuf(self, name, shape, dt):
        h = self.stack.enter_context(self.nc.sbuf_tensor(name, list(shape), dt))
        return T(h[:], name)

    def psum(self, name, shape, dt=F32):
        h = self.stack.enter_context(self.nc.psum_tensor(name, list(shape), dt))
        return T(h[:], name, True)

    def dram(self, name, shape, dt, kind="Internal"):
        h = self.nc.dram_tensor(name, list(shape), dt, kind=kind)
        return T(h.ap(), name if kind != "ExternalInput" else None)

    def op(self, eng, fn, r=(), w=(), dma=False):
        o = Op()
        o.eng, o.fn, o.dma, o.sig, o.tok, o.waits = eng, fn, dma, False, None, None
        o.cc = False
        o.gidx = len(self.ops)
        deps = set()
        w = list(w) + [t for t in r if t.excl]
        for t in r:
            k = t.key
            if k is None:
                continue
            lw = self.lastw.get(k)
            if lw is not None:
                deps.add(lw)
            self.readers.setdefault(k, []).append(o)
        for t in w:
            k = t.key
            if k is None:
                continue
            lw = self.lastw.get(k)
            if lw is not None:
                deps.add(lw)
            for rd in self.readers.get(k, ()):
                if rd is not o:
                    deps.add(rd)
            self.readers[k] = []
            self.lastw[k] = o
        pend = self.pending.pop(eng, None)
        if pend:
            deps |= pend
        deps.discard(o)
        o.deps = deps
        self.ops.append(o)
        self.last_on[eng] = o
        if dma:
            self.dma_hist[eng].append(o)
        return o

    def mm(self, out, lhsT, rhs, start=True, stop=True):
        r = [lhsT, rhs] + ([] if start else [out])
        self.op("pe", lambda e: e.matmul(out.ap, lhsT.ap, rhs.ap, start=start, stop=stop), r, [out])

    def tr(self, out, in_, ident):
        self.op("pe", lambda e: e.transpose(out.ap, in_.ap, ident.ap), [in_, ident], [out])

    def act(self, out, in_, func, bias=None, scale=None, accum_out=None, eng="act"):
        kw = {}
        r = [in_]
        w = [out]
        if bias is not None:
            if isinstance(bias, T):
                kw["bias"] = bias.ap
                r.append(bias)
            else:
                kw["bias"] = bias
        if scale is not None:
            if isinstance(scale, T):
                kw["scale"] = scale.ap
                r.append(scale)
            else:
                kw["scale"] = scale
        if accum_out is not None:
            kw["accum_out"] = accum_out.ap
            w.append(accum_out)
        self.op("act", lambda e: e.activation(out.ap, in_.ap, func, **kw), r, w)

    def tt(self, eng, out, in0, in1, op):
        self.op(eng, lambda e: e.tensor_tensor(out.ap, in0.ap, in1.ap, op), [in0, in1], [out])

    def ts(self, eng, out, in0, s1, s2=None, op0=ALU.mult, op1=None, accum_out=None):
        r = [in0]
        a1 = s1.ap if isinstance(s1, T) else s1
        a2 = s2.ap if isinstance(s2, T) else s2
        if isinstance(s1, T):
            r.append(s1)
        if isinstance(s2, T):
            r.append(s2)
        w = [out]
        kw = {}
        if op1 is not None:
            kw["op1"] = op1
        if accum_out is not None:
            kw["accum_out"] = accum_out.ap
            w.append(accum_out)
        self.op(eng, lambda e: e.tensor_scalar(out.ap, in0.ap, a1, a2, op0, **kw), r, w)

    def stt(self, eng, out, in0, scalar, in1, op0, op1):
        r = [in0, in1]
        a = scalar.ap if isinstance(scalar, T) else scalar
        if isinstance(scalar, T):
            r.append(scalar)
        self.op(eng, lambda e: e.scalar_tensor_tensor(out.ap, in0.ap, a, in1.ap, op0, op1), r, [out])

    def rsum(self, out, in_):
        self.op("dve", lambda e: e.reduce_sum(out.ap, in_.ap, AX.X), [in_], [out])

    def rmax(self, out, in_):
        self.op("dve", lambda e: e.reduce_max(out.ap, in_.ap, AX.X), [in_], [out])

    def scan(self, out, d0, d1, initial, op0, op1):
        r = [d0, d1]
        a = initial.ap if isinstance(initial, T) else initial
        if isinstance(initial, T):
            r.append(initial)
        self.op("dve", lambda e: e.tensor_tensor_scan(out.ap, d0.ap, d1.ap, a, op0, op1), r, [out])

    def recip(self, out, in_):
        self.op("dve", lambda e: e.reciprocal(out.ap, in_.ap), [in_], [out])

    def copy(self, eng, out, in_):
        if eng == "act":
            self.op("act", lambda e: e.activation(out.ap, in_.ap, AF.Copy), [in_], [out])
        else:
            self.op(eng, lambda e: e.tensor_copy(out.ap, in_.ap), [in_], [out])

    def memset(self, eng, out, val):
        self.op(eng, lambda e: e.memset(out.ap, val), [], [out])

    def cc(self, kind, groups, out, in_):
        o = self.op("pool", lambda e: e.collective_compute(kind, ALU.bypass, replica_groups=groups,
                                                            ins=[in_.ap.opt()], outs=[out.ap.opt()]),
                    [in_], [out], dma=True)
        o.cc = True
        return o

    def dma(self, q, out, in_, rkeys=None, wkeys=None, **kw):
        self.op(q, lambda e: e.dma_start(out=out.ap, in_=in_.ap, **kw),
                rkeys if rkeys is not None else [in_], wkeys if wkeys is not None else [out], dma=True)

    def emit(self):
        nc = self.nc
        st = self.stack
        per = {e: [] for e in self.ENGS}
        for o in self.ops:
            o.pos = len(per[o.eng])
            per[o.eng].append(o)

        def need(o, d):
            if d.dma:
                return True
            if d.eng != o.eng:
                return True
            if o.eng == "pe" and not o.dma:
                return False
            if o.dma:
                return True
            return (o.pos - d.pos) <= self.SAMEW

        for o in self.ops:
            o.waits = [d for d in o.deps if need(o, d)]
            for d in o.waits:
                d.sig = True
        esem = {e: st.enter_context(nc.semaphore("es_" + e)) for e in self.ENGS}
        rings = {}
        ecount = {e: 0 for e in self.ENGS}
        dcount = {e: 0 for e in self.ENGS}
        for e in self.ENGS:
            nd = sum(1 for o in per[e] if o.dma and not o.cc)
            if nd:
                rings[e] = [st.enter_context(nc.semaphore("ds_%s_%d" % (e, i)))
                            for i in range(min(self.RING, nd))]
        for e in self.ENGS:
            for o in per[e]:
                if o.cc:
                    o.tok = (st.enter_context(nc.semaphore("cc_%d" % o.gidx)), 1, 1, None)
                elif o.dma:
                    j = dcount[e]
                    dcount[e] += 1
                    R = len(rings[e])
                    o.tok = (rings[e][j % R], 16 * (j // R + 1), 16, j)
                elif o.sig:
                    ecount[e] += 1
                    o.tok = (esem[e], ecount[e], 1, None)

        def run(ename, eng):
            known = {}

            def wait(sem, val):
                if known.get(id(sem), 0) >= val:
                    return
                known[id(sem)] = val
                eng.wait_ge(sem, val)

            for o in per[ename]:
                if getattr(self, "dump", None) is not None:
                    self.dump.append((ename, o.gidx, o.dma, [(d.eng, d.gidx, d.dma, d.tok[1]) for d in sorted(o.waits, key=lambda d: d.gidx)], None if o.tok is None else o.tok[1]))
                for d in sorted(o.waits, key=lambda d: d.gidx):
                    wait(d.tok[0], d.tok[1])
                if o.dma and not o.cc:
                    sem, val, inc, j = o.tok
                    if val > 16:
                        wait(sem, val - 16)
                ins = o.fn(eng)
                if o.tok is not None:
                    ins.then_inc(o.tok[0], o.tok[2])
                if o.cc:
                    wait(o.tok[0], 1)
            if ename in rings:
                R = len(rings[ename])
                n = dcount[ename]
                for i in range(R):
                    cnt = (n - i + R - 1) // R if n > i else 0
                    if cnt:
                        wait(rings[ename][i], 16 * cnt)

        with nc.Block() as block:
            if per["pe"]:
                @block.tensor
                def _(e):
                    run("pe", e)
            if per["act"]:
                @block.scalar
                def _(e):
                    run("act", e)
            if per["dve"]:
                @block.vector
                def _(e):
                    run("dve", e)
            if per["pool"]:
                @block.gpsimd
                def _(e):
                    run("pool", e)
            if per["sp"]:
                @block.sync
                def _(e):
                    run("sp", e)
        self.stack.close()
        return {e: len(per[e]) for e in self.ENGS}
import math

D = 2048
FF = 5504
NFC = FF // 128
NDC = D // 128
TB = 512
DEPTH = 2
ALPHA = (2 * DEPTH) ** 0.25
LN_EPS = 1e-5


class Arena:
    def __init__(self, P, nbytes):
        self.P = P
        self.t = P.sbuf("arena", [128, nbytes // 4], F32)
        self.off = 0
        self.n = 0
        self.nbytes = nbytes

    def reset(self):
        self.off = 0

    def alloc(self, name, free_shape, dt):
        esz = 4 if dt == F32 else 2
        n = 1
        for s in free_shape:
            n *= s
        nb = (n * esz + 63) // 64 * 64
        assert self.off + nb <= self.nbytes, (name, self.off, nb)
        ap = self.t.ap[:, self.off // 4:(self.off + nb) // 4]
        if dt != F32:
            ap = ap.bitcast(dt)
        ap = ap[:, 0:n]
        if len(free_shape) == 2:
            ap = ap.rearrange("p (a b) -> p a b", a=free_shape[0])
        elif len(free_shape) == 3:
            ap = ap.rearrange("p (a b c) -> p a b c", a=free_shape[0], b=free_shape[1])
        self.off += nb
        self.n += 1
        return T(ap, "%s#%d" % (name, self.n))


BG = []


def bg_emit(n):
    for _ in range(min(n, len(BG))):
        BG.pop(0)()


def bg_precast(P, wgu, wd, wgu_bf, wd_bf):
    for j in range(NFC):
        for t in range(2):
            BG.append(lambda j=j, t=t: P.dma("pool", T(wgu_bf.ap[j, t], (wgu_bf.key, j, t)), T(wgu.ap[j, t], wgu.key)))
    for c in range(NDC):
        BG.append(lambda c=c: P.dma("pool", T(wd_bf.ap[c], (wd_bf.key, c)), T(wd.ap[c], wd.key)))


def down_ln(P, A, PS, CONST, rhsT, NK, wd, lcol, z, xres_in, xT_out, xres_out, out_tok, t0, bufs, wd_bf=None, first=0, stagger=False):
    ones = CONST["ones"]
    g_col, b_col = lcol
    wds, sq, mu, rstd, tmp, ob, otok = bufs
    P.dma("sp", z, T(xres_in.ap[:, t0:t0 + TB].rearrange("(c p) t -> p c t", p=128), xres_in.key),
          wkeys=[z.k(c) for c in range(NDC)])
    S1 = PS[6]
    S2 = PS[7]

    def stats(c):
        P.mm(S1, ones, z[:, c, :].k(c), start=(c == 0), stop=(c == NDC - 1))
        q = sq[c % 2]
        P.act(q, z[:, c, :].k(c), AF.Square)
        P.mm(S2, ones, q, start=(c == 0), stop=(c == NDC - 1))

    for c in range(NDC):
        slot = wds[down_ln.di % len(wds)]
        down_ln.di += 1
        sflat = T(slot.ap[:, 0:NK, :].rearrange("p j m -> p (j m)"), slot.key)
        fill_blk = (c % 2) if stagger else 0
        if wd_bf is None or (0 <= first <= fill_blk):
            P.dma("pool", sflat, T(wd.ap[c], wd.key))
            if wd_bf is not None and first == fill_blk:
                P.dma("sp", T(wd_bf.ap[c], (wd_bf.key, c)), sflat)
        else:
            P.dma("sp", sflat, T(wd_bf.ap[c], (wd_bf.key, c)))
        py = PS[4 + (c % 2)]
        for j in range(NK):
            P.mm(py, slot[:, j, :], rhsT[:, j, :], start=(j == 0), stop=(j == NK - 1))
        zc = z[:, c, :].k(c)
        P.stt("dve", zc, zc, ALPHA, py, ALU.mult, ALU.add)
        if c >= 1:
            stats(c - 1)
    stats(NDC - 1)
    P.act(mu, S1, AF.Copy, scale=1.0 / D)
    P.tt("dve", tmp[0], mu, mu, ALU.mult)
    P.stt("dve", tmp[1], S2, 1.0 / D, tmp[0], ALU.mult, ALU.subtract)
    P.ts("dve", tmp[0], tmp[1], LN_EPS, None, ALU.add)
    P.act(tmp[1], tmp[0], AF.Sqrt)
    P.recip(rstd, tmp[1])
    for c in range(NDC):
        zc = z[:, c, :].k(c)
        tt_ = tmp[c % 2]
        P.tt("dve", tt_, zc, mu, ALU.subtract)
        P.tt("dve", tt_, tt_, rstd, ALU.mult)
        P.ts("dve", zc, tt_, g_col[:, c:c + 1], b_col[:, c:c + 1], ALU.mult, ALU.add)
        if out_tok is None:
            P.copy("act", ob[:, c, :].k(c), zc)
    if out_tok is None:
        P.dma("sp", T(xres_out.ap[:, t0:t0 + TB].rearrange("(c p) t -> p c t", p=128), xres_out.key), z,
              rkeys=[z.k(c) for c in range(NDC)])
        P.dma("sp", T(xT_out.ap[:, t0:t0 + TB].rearrange("(c p) t -> p c t", p=128), xT_out.key), ob,
              rkeys=[ob.k(c) for c in range(NDC)])
    else:
        ident_f = CONST["ident_f"]
        for tt in range(TB // 128):
            for c4 in range(NDC // 4):
                pt = PS[c4 % 4]
                for cc in range(4):
                    c = c4 * 4 + cc
                    P.mm(pt[:, cc * 128:(cc + 1) * 128], z[:, c, tt * 128:(tt + 1) * 128].k(c), ident_f)
                P.copy("act" if c4 % 2 else "dve", otok[:, c4 * 512:(c4 + 1) * 512].k(c4), pt)
            P.dma("sp", T(out_tok.ap[t0 + tt * 128:t0 + (tt + 1) * 128, :], out_tok.key), otok,
                  rkeys=[otok.k(i) for i in range(4)])


down_ln.di = 0


def ln_bufs(A, NKmax):
    wds = [A.alloc("wd%d" % i, [NKmax, 128], BF16) for i in range(3)]
    sq = [A.alloc("sq%d" % i, [TB], F32) for i in range(2)]
    mu = A.alloc("mu", [TB], F32)
    rstd = A.alloc("rstd", [TB], F32)
    tmp = [A.alloc("tmp%d" % i, [TB], F32) for i in range(2)]
    return wds, sq, mu, rstd, tmp


def ffn_phase(P, A, PS, CONST, xT_in, xres_in, wgu, wd, lcol, xT_out, xres_out, NT, out_tok=None, wgu_bf=None, wd_bf=None,
              cached=False):
    P.fence()
    A.reset()
    xb = A.alloc("xb", [NDC, TB], BF16)
    gT = A.alloc("gT", [NFC, TB], BF16)
    z = A.alloc("z", [NDC, TB], F32)
    NWG = 4
    wgs = [A.alloc("wg%d" % i, [2, NDC, 128], BF16) for i in range(NWG)]
    sg = [A.alloc("sg%d" % i, [TB], F32) for i in range(2)]
    wds, sq, mu, rstd, tmp = ln_bufs(A, NFC)
    if out_tok is None:
        ob = A.alloc("ob", [NDC, TB], BF16)
        otok = None
    else:
        ob = None
        otok = A.alloc("otok", [D], F32)
    bufs = (wds, sq, mu, rstd, tmp, ob, otok)
    wi = 0
    nblk_total = NT // TB
    for blk in range(NT // TB):
        t0 = blk * TB
        P.dma("sp", xb, T(xT_in.ap[:, t0:t0 + TB].rearrange("(c p) t -> p c t", p=128), xT_in.key))
        for j in range(NFC):
            slot = wgs[wi % NWG]
            wi += 1
            for t in range(2):
                sflat = T(slot.ap[:, t].rearrange("p c f -> p (c f)"), (slot.key, t))
                fill_blk = (j % 2) if nblk_total > 1 else 0
                if wgu_bf is None or (blk <= fill_blk and not cached):
                    P.dma("pool", sflat, T(wgu.ap[j, t], wgu.key))
                    if wgu_bf is not None and blk == fill_blk:
                        P.dma("sp", T(wgu_bf.ap[j, t], (wgu_bf.key, j, t)), sflat)
                else:
                    P.dma("sp", sflat, T(wgu_bf.ap[j, t], (wgu_bf.key, j, t)))
            pa = PS[(j % 2) * 2]
            pu = PS[(j % 2) * 2 + 1]
            for t, ps in ((0, pa), (1, pu)):
                for c in range(NDC):
                    P.mm(ps, slot[:, t, c, :].k(t), xb[:, c, :], start=(c == 0), stop=(c == NDC - 1))
            s_ = sg[j % 2]
            P.act(s_, pa, AF.Silu)
            P.stt("dve", gT[:, j, :], s_, 0.5, pu, ALU.mult, ALU.mult)
        down_ln(P, A, PS, CONST, gT, NFC, wd, lcol, z, xres_in, xT_out, xres_out, out_tok, t0, bufs,
                wd_bf=wd_bf, first=(-1 if cached else blk), stagger=(nblk_total > 1))


def outproj_phase(P, A, PS, CONST, mT_in, xres_in, wo, lcol, xT_out, xres_out, NT):
    P.fence()
    A.reset()
    mb = [A.alloc("mb%d" % i, [NDC, TB], BF16) for i in range(2)]
    z = A.alloc("z", [NDC, TB], F32)
    ob = A.alloc("ob", [NDC, TB], BF16)
    wds, sq, mu, rstd, tmp = ln_bufs(A, NDC)
    bufs = (wds, sq, mu, rstd, tmp, ob, None)
    for blk in range(NT // TB):
        t0 = blk * TB
        m = mb[blk % 2]
        P.dma("sp", m, T(mT_in.ap[:, t0:t0 + TB].rearrange("(c p) t -> p c t", p=128), mT_in.key))
        down_ln(P, A, PS, CONST, m, NDC, wo, lcol, z, xres_in, xT_out, xres_out, None, t0, bufs)


def prologue_phase(P, A, PS, CONST, x_tok, xT_out, xres_out, NT):
    P.fence()
    A.reset()
    ident_f = CONST["ident_f"]
    xt = [A.alloc("xt%d" % i, [D], F32) for i in range(4)]
    z = A.alloc("z", [NDC, TB], F32)
    ob = A.alloc("ob", [NDC, TB], BF16)
    for blk in range(NT // TB):
        t0 = blk * TB
        for tt in range(4):
            P.dma("sp", xt[tt], T(x_tok.ap[t0 + tt * 128:t0 + (tt + 1) * 128, :], None))
        for c in range(NDC):
            pt = PS[c % 4]
            for tt in range(4):
                P.mm(pt[:, tt * 128:(tt + 1) * 128], xt[tt][:, c * 128:(c + 1) * 128], ident_f)
            P.copy("dve", z[:, c, :].k(c), pt)
            P.copy("act", ob[:, c, :].k(c), z[:, c, :].k(c))
        P.dma("sp", T(xres_out.ap[:, t0:t0 + TB].rearrange("(c p) t -> p c t", p=128), xres_out.key), z,
              rkeys=[z.k(c) for c in range(NDC)])
        P.dma("sp", T(xT_out.ap[:, t0:t0 + TB].rearrange("(c p) t -> p c t", p=128), xT_out.key), ob,
              rkeys=[ob.k(c) for c in range(NDC)])


HN_EPS = 1e-6
BG_PER_SLOT = 3
DBG = 0
DBG2 = 0
NRET = 2
KTM_ENG = "dve"


def bfview(ps):
    return T(ps.ap.bitcast(BF16), ps.key)


def load_w(P, slot, wm, col0, ncols, key=None):
    src = wm.ap[:, col0:col0 + ncols].rearrange("(c p) f -> p c f", p=128)
    P.dma("pool", slot if key is None else slot.k(key), T(src, wm.key))


def proj_fm(P, ps, w, xb, key=None):
    for c in range(NDC):
        wc = w[:, c, :] if key is None else w[:, c, :].k(key)
        P.mm(ps, wc, xb[:, c, :], start=(c == 0), stop=(c == NDC - 1))


def proj_tm(P, ps, w, xb, tt, key=None):
    for c in range(NDC):
        wc = w[:, c, :] if key is None else w[:, c, :].k(key)
        P.mm(ps, xb[:, c, tt * 128:(tt + 1) * 128], wc, start=(c == 0), stop=(c == NDC - 1))


def rstd_col(P, out, var, eps, tmp):
    P.act(tmp, var, AF.Ln, bias=eps)
    P.act(out, tmp, AF.Exp, scale=-0.5)


def head_norm_ln(P, A_, y_ps, ncol, st, mv, tmpc, rs):
    P.op("dve", lambda e: e.bn_stats(st.ap, y_ps.ap), [y_ps], [st])
    P.op("dve", lambda e: e.bn_aggr(mv.ap, st.ap), [st], [mv])
    rstd_col(P, rs, mv[:, 1:2], HN_EPS, tmpc)
    return mv[:, 0:1], rs


def ret_stage(P, A, PS, K, xblk, wm, col0, tabs, gn_bc, out_rows, S, gC):
    P.fence()
    A.reset()
    nblk = S // TB
    ws = [A.alloc("rw%d" % i, [NDC, 128], BF16) for i in range(4)]
    wvg = A.alloc("rwvg", [NDC, 256], BF16)
    for i in range(4):
        load_w(P, ws[i], wm, col0 + i * 128, 128)
    load_w(P, wvg, wm, col0 + 512, 256)
    xb = [A.alloc("xb%d" % i, [NDC, TB], BF16) for i in range(2)]
    tb4 = [A.alloc("tab%d" % i, [4, TB], F32) for i in range(2)]
    QT = [A.alloc("QT%d" % i, [TB], BF16) for i in range(2)]
    KT = [A.alloc("KT%d" % i, [TB], BF16) for i in range(2)]
    V = [A.alloc("V%d" % i, [4, 128], BF16) for i in range(2)]
    G = [A.alloc("G%d" % i, [4, 128], F32) for i in range(2)]
    Ktm = [A.alloc("Ktm%d" % i, [4, 128], BF16) for i in range(2)]
    outT = [A.alloc("outT%d" % i, [TB], BF16) for i in range(2)]
    R32 = A.alloc("R32", [128], F32)
    Rbf = A.alloc("Rbf", [128], BF16)
    tmpR = A.alloc("tmpR", [128], F32)
    t1 = [A.alloc("t1_%d" % i, [TB], F32) for i in range(2)]
    t2 = [A.alloc("t2_%d" % i, [TB], F32) for i in range(2)]
    ATm = [A.alloc("ATm%d" % i, [128], BF16) for i in range(2)]
    yn = [A.alloc("yn%d" % i, [128], F32) for i in range(2)]
    ob = [A.alloc("yo%d" % i, [128], BF16) for i in range(2)]
    st = [A.alloc("st%d" % i, [6], F32) for i in range(2)]
    mv = [A.alloc("mv%d" % i, [2], F32) for i in range(2)]
    tc_ = [A.alloc("tc%d" % i, [1], F32) for i in range(2)]
    rs = [A.alloc("rs%d" % i, [1], F32) for i in range(2)]
    P.memset("dve", R32, 0.0)
    P.memset("dve", Rbf, 0.0)
    maskT = K["maskT_f"]
    ident = K["ident_b"]
    ci = 0
    for tb in range(nblk):
        t0 = tb * TB
        x = xb[tb % 2]
        xblk(x, tb)
        tab = tb4[tb % 2]
        P.dma("sp", tab, T(tabs.ap[:, :, t0:t0 + TB].rearrange("f p t -> p f t"), None))
        q, k = QT[tb % 2], KT[tb % 2]
        for which, dst in ((0, q), (1, k)):
            pa, pb = PS[which * 2], PS[which * 2 + 1]
            proj_fm(P, pa, ws[which * 2], x)
            proj_fm(P, pb, ws[which * 2 + 1], x)
            a1, a2 = t1[which], t2[which]
            P.tt("dve", a1, pa, tab[:, which * 2, :], ALU.mult)
            P.tt("dve", a2, pb, tab[:, which * 2 + 1, :], ALU.mult)
            P.tt("dve", dst, a1, a2, ALU.add)
        v, g, ktm = V[tb % 2], G[tb % 2], Ktm[tb % 2]
        if DBG == 1:
            P.dma("sp", T(out_rows.ap[:, t0:t0 + TB], out_rows.key), q)
            continue
        for tt in range(4):
            if DBG2 & 1:
                break
            pv = PS[4]
            proj_tm(P, pv[:, 0:256], wvg, x, tt)
            if DBG2 & 2:
                continue
            P.copy("dve", v[:, tt, :].k(tt), pv[:, 0:128])
            if DBG2 & 4:
                continue
            P.act(g[:, tt, :].k(tt), pv[:, 128:256], AF.Silu)
        pT = PS[5]
        for tt in range(4):
            if DBG2 & 8:
                break
            pTt = pT[:, tt * 128:(tt + 1) * 128]
            P.mm(pTt, (q if DBG2 & 64 else k)[:, tt * 128:(tt + 1) * 128], ident)
            if DBG2 & 16:
                continue
            P.copy(KTM_ENG, ktm[:, tt, :].k(tt), pTt)
        oT = outT[tb % 2]
        if DBG == 2:
            P.dma("sp", T(out_rows.ap[:, t0:t0 + TB], out_rows.key), q, rkeys=[q] + [ktm[:, 0, :].k(i) for i in range(4)])
            continue
        for tt in range(4):
            sl = slice(tt * 128, (tt + 1) * 128)
            pAT = PS[6][:, 0:128]
            py = PS[7][:, 0:128]
            pU = PS[5][:, 0:128]
            P.mm(pAT, k[:, sl], q[:, sl])
            am = ATm[ci % 2]
            P.tt("dve", am, pAT, maskT, ALU.mult)
            P.mm(py, am, v[:, tt, :].k(tt), start=True, stop=False)
            P.mm(py, q[:, sl], Rbf, start=False, stop=True)
            P.mm(pU, ktm[:, tt, :].k(tt), v[:, tt, :].k(tt))
            P.tt("dve", tmpR, pU, R32, ALU.add)
            P.ts("dve", R32, tmpR, gC, None, ALU.mult)
            P.ts("dve", Rbf, tmpR, gC, None, ALU.mult)
            i2 = ci % 2
            mean, rstd = head_norm_ln(P, A, py, 128, st[i2], mv[i2], tc_[i2], rs[i2])
            P.ts("dve", yn[i2], py, mean, rstd, ALU.subtract, ALU.mult)
            P.tt("dve", yn[i2], yn[i2], gn_bc, ALU.mult)
            P.tt("dve", ob[i2], yn[i2], g[:, tt, :].k(tt), ALU.mult)
            po = PS[4][:, 256:384]
            P.mm(po, ob[i2], ident)
            P.copy("dve", oT[:, sl].k(tt), po)
            ci += 1
        P.dma("sp", T(out_rows.ap[:, t0:t0 + TB], out_rows.key), oT, rkeys=[oT[:, 0:1].k(tt) for tt in range(4)])


def diff_stage(P, A, PS, K, xblk, wm, col0, tabs, gd_bc, neglam, out_rows, S):
    P.fence()
    A.reset()
    nblk = S // TB
    NT_ = S // 128
    ws = [A.alloc("dw%d" % i, [NDC, 128], BF16) for i in range(8)]
    wv = A.alloc("dwv", [NDC, 256], BF16)
    for i in range(0, 8, 2):
        load_w(P, ws[i], wm, col0 + i * 128, 128)
    load_w(P, wv, wm, col0 + 1024, 256)
    xb = [A.alloc("xb%d" % i, [NDC, TB], BF16) for i in range(2)]
    tb2 = [A.alloc("tab%d" % i, [2, TB], F32) for i in range(2)]
    QK = [A.alloc("QK%d" % i, [S], BF16) for i in range(4)]
    Va = A.alloc("Va", [NT_, 257], BF16)
    t1 = [A.alloc("t1_%d" % i, [TB], F32) for i in range(2)]
    t2 = [A.alloc("t2_%d" % i, [TB], F32) for i in range(2)]
    E = [A.alloc("E%d" % i, [TB], BF16) for i in range(4)]
    o1 = A.alloc("o1", [4, 256], F32)
    o2 = A.alloc("o2", [4, 256], F32)
    sqj = A.alloc("sqj", [256], F32)
    obf = [A.alloc("obf%d" % i, [256], BF16) for i in range(2)]
    outT = [A.alloc("outT%d" % i, [2, TB], BF16) for i in range(2)]
    cols = [A.alloc("cols%d" % i, [4], F32) for i in range(8)]
    ident = K["ident_b"]
    dmask = K["dmask"]
    P.op("dve", lambda e: e.memset(Va.ap[:, :, 256:257], 1.0), [], [Va.k(i) for i in range(NT_)])
    for tb in range(nblk):
        t0 = tb * TB
        x = xb[tb % 2]
        xblk(x, tb)
        tab = tb2[tb % 2]
        P.dma("sp", tab, T(tabs.ap[:, :, t0:t0 + TB].rearrange("f p t -> p f t"), None))
        bg_emit(BG_PER_SLOT)
        for w in range(4):
            pa, pb = PS[(w % 2) * 2], PS[(w % 2) * 2 + 1]
            proj_fm(P, pa, ws[w * 2], x)
            a1, a2 = t1[w % 2], t2[w % 2]
            P.act(a1, pa, AF.Copy)
            P.mm(pb, K["pm_d"], a1)
            P.tt("dve", a2, pb, tab[:, 1, :], ALU.mult)
            P.tt("dve", a1, a1, tab[:, 0, :], ALU.mult)
            P.tt("dve", QK[w][:, t0:t0 + TB].k(tb), a1, a2, ALU.add)
        for tt in range(4):
            pv = PS[4 + tt % 2]
            proj_tm(P, pv[:, 0:256], wv, x, tt)
            P.copy("act", Va[:, tb * 4 + tt, 0:256].k(tb * 4 + tt), pv[:, 0:256])
    scale = 128.0 ** -0.5
    ei = 0
    for qt in range(nblk):
        q0 = qt * TB
        bg_emit(BG_PER_SLOT)
        for m in range(2):
            qT = QK[m][:, q0:q0 + TB].k(qt)
            acc = [PS[2 + i] for i in range(4)]
            nkb = 4 * qt + 4
            SB = (PS[0], PS[1], PS[7])
            DPT = 2
            ebuf = {}

            def score(kb):
                pS = SB[kb % 3]
                P.mm(pS, QK[2 + m][:, kb * 128:(kb + 1) * 128].k(kb // 4), qT)
                e = E[kb % 4]
                ebuf[kb] = e
                P.act(e, pS, AF.Exp, scale=scale)
                r = kb - 4 * qt
                if r >= 0:
                    P.tt("dve", e, e, dmask[:, 384 - 128 * r: 384 - 128 * r + TB], ALU.mult)

            def pv(kb):
                e = ebuf.pop(kb)
                r = kb - 4 * qt
                for qi in range(4):
                    if r > qi:
                        continue
                    last = (kb == min(nkb - 1, 4 * qt + qi))
                    P.mm(acc[qi][:, 0:257], e[:, qi * 128:(qi + 1) * 128], Va[:, kb, :].k(kb),
                         start=(kb == 0), stop=last)

            for i in range(nkb + DPT):
                if i < nkb:
                    score(i)
                if i >= DPT:
                    pv(i - DPT)
            c = cols[m]
            for qi in range(4):
                P.recip(c[:, qi:qi + 1].k(qi), acc[qi][:, 256:257])
                if m == 0:
                    P.ts("dve", o1[:, qi, :].k(qi), acc[qi][:, 0:256], c[:, qi:qi + 1].k(qi), None, ALU.mult)
                else:
                    P.ts("dve", o2[:, qi, :].k(qi), acc[qi][:, 0:256], c[:, qi:qi + 1].k(qi), None, ALU.mult)
        oT = outT[qt % 2]
        for qi in range(4):
            o = o1[:, qi, :].k(qi)
            P.stt("dve", o, o2[:, qi, :].k(qi), neglam, o, ALU.mult, ALU.add)
            ss = cols[2][:, qi:qi + 1].k(qi)
            P.tt("dve", sqj, o, o, ALU.mult)
            P.rsum(ss, sqj)
            tcol = cols[3][:, qi:qi + 1].k(qi)
            rcol = cols[4][:, qi:qi + 1].k(qi)
            P.ts("dve", tcol, ss, 1.0 / 256, HN_EPS, ALU.mult, ALU.add)
            P.act(tcol, tcol, AF.Ln)
            P.act(rcol, tcol, AF.Exp, scale=-0.5)
            ob_ = obf[qi % 2]
            P.stt("dve", ob_, o, rcol, gd_bc, ALU.mult, ALU.mult)
            for hh in range(2):
                ri = qi * 2 + hh
                po = PS[6][:, (ri % 4) * 128:(ri % 4 + 1) * 128]
                P.mm(po, ob_[:, hh * 128:(hh + 1) * 128], ident)
                P.copy("dve", oT[:, hh, qi * 128:(qi + 1) * 128].k(qi * 2 + hh), po)
        P.dma("sp", T(out_rows.ap[:, q0:q0 + TB].rearrange("(h p) t -> p h t", p=128), out_rows.key), oT,
              rkeys=[oT[:, 0, 0:1].k(i) for i in range(8)])


def mlstm_stage(P, A, PS, K, xblk, wm, col0, cw, gb, gm_bc, out_rows, S):
    P.fence()
    A.reset()
    nblk = S // TB
    NT_ = S // 128
    R = [A.alloc("R%d" % i, [S + 8], F32) for i in range(4)]
    qT = A.alloc("qT", [S], BF16)
    kT = A.alloc("kT", [S], BF16)
    qpT = A.alloc("qpT", [S], BF16)
    Va = A.alloc("Va", [NT_, 129], BF16)
    Og = A.alloc("Og", [NT_, 128], BF16)
    mark0 = A.off
    wq = A.alloc("mwq", [NDC, 128], BF16)
    wk = A.alloc("mwk", [NDC, 128], BF16)
    wi = A.alloc("mwi", [NDC, 128], BF16)
    wf = A.alloc("mwf", [NDC, 128], BF16)
    wvo = A.alloc("mwvo", [NDC, 256], BF16)
    for i, sl in enumerate((wq, wk, wi, wf)):
        load_w(P, sl, wm, col0 + i * 128, 128)
    load_w(P, wvo, wm, col0 + 512, 256)
    xb = [A.alloc("xb%d" % i, [NDC, TB], BF16) for i in range(2)]
    ident = K["ident_b"]
    ident_f = K["ident_f"]
    maskT = K["maskT_f"]
    qraw, kraw, Ipre, Fpre = R
    P.memset("dve", qraw[:, 0:8], 0.0)
    P.memset("dve", kraw[:, 0:8], 0.0)
    P.op("dve", lambda e: e.memset(Va.ap[:, :, 128:129], 1.0), [], [Va.k(i) for i in range(NT_)])
    for tb in range(nblk):
        t0 = tb * TB
        x = xb[tb % 2]
        xblk(x, tb)
        bg_emit(BG_PER_SLOT)
        for i, (w, dst, off) in enumerate(((wq, qraw, 3), (wk, kraw, 3), (wi, Ipre, 0), (wf, Fpre, 0))):
            ps = PS[i]
            proj_fm(P, ps, w, x)
            P.copy("act" if i % 2 else "dve", dst[:, off + t0: off + t0 + TB], ps)
        for tt in range(4):
            pv = PS[4 + tt % 2]
            proj_tm(P, pv[:, 0:256], wvo, x, tt)
            P.copy("dve", Va[:, tb * 4 + tt, 0:128].k(tb * 4 + tt), pv[:, 0:128])
            P.act(Og[:, tb * 4 + tt, :].k(tb * 4 + tt), pv[:, 128:256], AF.Sigmoid)
    P.fence()
    A.off = mark0
    NB = A.alloc("NB", [S + 8], F32)
    Mx = A.alloc("Mx", [S + 8], F32)
    onesr = A.alloc("onesr", [TB], F32)
    colG = A.alloc("colG", [NT_], F32)
    colEW = A.alloc("colEW", [NT_], F32)
    colENM = A.alloc("colENM", [NT_], F32)
    Mc = A.alloc("Mc", [NT_ + 1], F32)
    NMc = A.alloc("NMc", [NT_ + 1], F32)
    ngb = A.alloc("ngb", [2], F32)
    DEC = A.alloc("DEC", [NT_], F32)
    junk = A.alloc("junk", [NT_, 128], F32)
    S32 = A.alloc("S32", [129], F32)
    Sbf = A.alloc("Sbf", [129], BF16)
    WT = [A.alloc("WT%d" % i, [128], F32) for i in range(2)]
    WTm = [A.alloc("WTm%d" % i, [128], F32) for i in range(2)]
    scT = [A.alloc("scT%d" % i, [128], BF16) for i in range(2)]
    KW = [A.alloc("KW%d" % i, [128], BF16) for i in range(2)]
    hh = [A.alloc("hh%d" % i, [128], F32) for i in range(2)]
    hb = [A.alloc("hb%d" % i, [128], BF16) for i in range(2)]
    st = [A.alloc("st%d" % i, [6], F32) for i in range(2)]
    mv = [A.alloc("mv%d" % i, [2], F32) for i in range(2)]
    tc_ = [A.alloc("tc%d" % i, [1], F32) for i in range(2)]
    rs = [A.alloc("rs%d" % i, [1], F32) for i in range(2)]
    dcol = [A.alloc("dcol%d" % i, [1], F32) for i in range(2)]
    outT = [A.alloc("outT%d" % i, [TB], BF16) for i in range(2)]
    P.memset("dve", onesr, 1.0)
    P.memset("dve", S32, 0.0)
    P.memset("dve", Sbf, 0.0)
    P.ts("dve", ngb, gb, -1.0, None, ALU.mult)
    for (raw, dst, wofs, bofs, scl) in ((qraw, qT, 0, 8, None), (kraw, kT, 4, 9, 128.0 ** -0.5)):
        for hb_ in range(0, S, 2048):
            n = min(2048, S - hb_)
            acc = NB[:, hb_:hb_ + n]
            P.ts("dve", acc, raw[:, hb_:hb_ + n], cw[:, wofs:wofs + 1], cw[:, bofs:bofs + 1], ALU.mult, ALU.add)
            for j in range(1, 4):
                P.stt("dve", acc, raw[:, hb_ + j:hb_ + j + n], cw[:, wofs + j:wofs + j + 1], acc, ALU.mult, ALU.add)
            if scl is None:
                P.act(dst[:, hb_:hb_ + n], acc, AF.Silu)
            else:
                P.act(acc, acc, AF.Silu)
                P.act(dst[:, hb_:hb_ + n], acc, AF.Copy, scale=scl)
    Lr = Fpre
    P.act(Lr[:, 0:S], Fpre[:, 0:S], AF.Exp, scale=-1.0, bias=ngb[:, 1:2])
    P.act(Lr[:, 0:S], Lr[:, 0:S], AF.Ln, bias=1.0)
    for tb in range(nblk):
        t0 = tb * TB
        init = 0.0 if tb == 0 else NB[:, t0 - 1:t0]
        P.scan(NB[:, t0:t0 + TB], onesr, Lr[:, t0:t0 + TB], init, ALU.mult, ALU.add)
    G = Ipre
    P.stt("dve", G[:, 0:S], Ipre[:, 0:S], gb[:, 0:1], NB[:, 0:S], ALU.add, ALU.add)
    for tb in range(nblk):
        t0 = tb * TB
        init = 0.0 if tb == 0 else Mx[:, t0 - 1:t0]
        P.scan(Mx[:, t0:t0 + TB], G[:, t0:t0 + TB], G[:, t0:t0 + TB], init, ALU.max, ALU.max)
    P.memset("dve", Mc[:, 0:1], 0.0)
    P.copy("dve", Mc[:, 1:NT_ + 1], T(Mx.ap[:, 0:S].rearrange("p (n c) -> p n c", c=128)[:, :, 127], Mx.key))
    P.ts("dve", NMc, Mc, -1.0, None, ALU.mult)
    P.tt("dve", DEC, Mc[:, 0:NT_], Mc[:, 1:NT_ + 1], ALU.subtract)
    P.act(DEC, DEC, AF.Exp)
    Mx3 = T(Mx.ap[:, 0:S].rearrange("p (n c) -> p n c", c=128), Mx.key)
    G3 = T(G.ap[:, 0:S].rearrange("p (n c) -> p n c", c=128), G.key)
    NB3 = T(NB.ap[:, 0:S].rearrange("p (n c) -> p n c", c=128), NB.key)
    EW = qraw
    ENM = kraw
    EIN = Fpre
    EW3 = T(EW.ap[:, 0:S].rearrange("p (n c) -> p n c", c=128), EW.key)
    ENM3 = T(ENM.ap[:, 0:S].rearrange("p (n c) -> p n c", c=128), ENM.key)
    EIN3 = T(EIN.ap[:, 0:S].rearrange("p (n c) -> p n c", c=128), EIN.key)
    bc = lambda t_, lo: T(t_.ap[:, lo:lo + NT_].unsqueeze(2).to_broadcast([128, NT_, 128]), t_.key)
    P.tt("dve", EW3, G3, bc(NMc, 1), ALU.add)
    P.act(EW[:, 0:S], EW[:, 0:S], AF.Exp)
    P.tt("dve", EIN3, bc(Mc, 0), Mx3, ALU.subtract)
    P.act(EIN[:, 0:S], EIN[:, 0:S], AF.Exp)
    P.tt("dve", ENM[:, 0:S], NB[:, 0:S], Mx[:, 0:S], ALU.subtract)
    P.act(ENM[:, 0:S], ENM[:, 0:S], AF.Exp)
    P.tt("dve", qpT, qT, EIN[:, 0:S], ALU.mult)
    ident_bc = T(ident_f.ap.unsqueeze(1).to_broadcast([128, NT_, 128]), ident_f.key)
    for (src3, col) in ((G3, colG), (EW3, colEW), (ENM3, colENM)):
        P.tt("dve", junk, src3, ident_bc, ALU.mult)
        P.rsum(col, junk)
    def front(n):
        sl = slice(n * 128, (n + 1) * 128)
        i2 = n % 2
        pS = PS[n % 2][:, 0:128]
        pN = PS[2 + n % 2][:, 0:129]
        pU = PS[4 + n % 2][:, 0:129]
        P.ts("dve", WT[i2], Mx[:, sl], colG[:, n:n + 1], 0.0, ALU.subtract, ALU.max)
        P.mm(pS, kT[:, sl], qT[:, sl])
        yield
        P.act(WT[i2], WT[i2], AF.Exp, scale=-1.0)
        pk = PS[6][:, 0:128]
        P.mm(pk, kT[:, sl], ident)
        yield
        P.tt("dve", WTm[i2], WT[i2], maskT, ALU.mult)
        P.ts("dve", KW[i2], pk, colEW[:, n:n + 1], None, ALU.mult)
        yield
        P.tt("dve", scT[i2], pS, WTm[i2], ALU.mult)
        yield
        P.mm(pN, scT[i2], Va[:, n, :].k(n), start=True, stop=False)
        P.mm(pN, qpT[:, sl], Sbf, start=False, stop=True)
        P.mm(pU, KW[i2], Va[:, n, :].k(n))
        yield
        P.stt("dve", S32, S32, DEC[:, n:n + 1], pU, ALU.mult, ALU.add)
        yield
        P.copy("dve", Sbf, S32)

    def back(n):
        i2 = n % 2
        pN = PS[2 + n % 2][:, 0:129]
        P.copy("dve", tc_[i2], pN[:, 128:129])
        yield
        P.stt("dve", dcol[i2], tc_[i2], -1.0, tc_[i2], ALU.mult, ALU.max)
        yield
        P.ts("dve", dcol[i2], dcol[i2], colENM[:, n:n + 1], None, ALU.max)
        yield
        P.recip(dcol[i2], dcol[i2])
        yield
        P.stt("dve", hh[i2], pN[:, 0:128], dcol[i2], Og[:, n, :].k(n), ALU.mult, ALU.mult)
        yield
        P.op("dve", lambda e: e.bn_stats(st[i2].ap, hh[i2].ap), [hh[i2]], [st[i2]])
        yield
        P.op("dve", lambda e: e.bn_aggr(mv[i2].ap, st[i2].ap), [st[i2]], [mv[i2]])
        yield
        P.act(tc_[i2], mv[i2][:, 1:2], AF.Ln, bias=HN_EPS)
        yield
        P.act(rs[i2], tc_[i2], AF.Exp, scale=-0.5)
        yield
        P.ts("dve", hh[i2], hh[i2], mv[i2][:, 0:1], rs[i2], ALU.subtract, ALU.mult)
        yield
        P.tt("dve", hb[i2], hh[i2], gm_bc, ALU.mult)
        yield
        po = PS[7][:, 0:128]
        P.mm(po, hb[i2], ident)
        yield
        oT = outT[(n // 4) % 2]
        P.copy("dve", oT[:, (n % 4) * 128:(n % 4 + 1) * 128].k(n % 4), po)
        if n % 4 == 3:
            t0 = (n // 4) * TB
            P.dma("sp", T(out_rows.ap[:, t0:t0 + TB], out_rows.key), oT, rkeys=[oT[:, 0:1].k(i) for i in range(4)])

    for n in range(NT_ + 1):
        gens = []
        if n % 4 == 0:
            bg_emit(BG_PER_SLOT)
        if n < NT_:
            gens.append(front(n))
        if n >= 1:
            gens.append(back(n - 1))
        while gens:
            for gnr in list(gens):
                try:
                    next(gnr)
                except StopIteration:
                    gens.remove(gnr)


def ret_stage2(P, A, PS, K, xblk, wm, col0s, tabs2, gn_bcs, out_rows2, S, gCs):
    P.fence()
    A.reset()
    nblk = S // TB
    H = 2
    ws = [[A.alloc("rw%d_%d" % (h, i), [NDC, 128], BF16) for i in range(4)] for h in range(H)]
    wvg = [A.alloc("rwvg%d" % h, [NDC, 256], BF16) for h in range(H)]
    for h in range(H):
        for i in (0, 2):
            load_w(P, ws[h][i], wm, col0s[h] + i * 128, 128)
        load_w(P, wvg[h], wm, col0s[h] + 512, 256)
    xb = [A.alloc("xb%d" % i, [NDC, TB], BF16) for i in range(2)]
    al2 = lambda nm, shp, dt: [[A.alloc("%s%d_%d" % (nm, h, i), shp, dt) for i in range(2)] for h in range(H)]
    tb4 = al2("tab", [4, TB], F32)
    QT = al2("QT", [TB], BF16)
    KT = al2("KT", [TB], BF16)
    V = al2("V", [4, 128], BF16)
    G = al2("G", [4, 128], F32)
    Ktm = al2("Ktm", [4, 128], BF16)
    outT = al2("outT", [TB], BF16)
    ATm = al2("ATm", [128], BF16)
    yn = al2("yn", [128], F32)
    ob = al2("yo", [128], BF16)
    st = al2("st", [6], F32)
    mv = al2("mv", [2], F32)
    tc_ = al2("tc", [1], F32)
    rs = al2("rs", [1], F32)
    R32 = [A.alloc("R32_%d" % h, [128], F32) for h in range(H)]
    Rbf = [A.alloc("Rbf_%d" % h, [128], BF16) for h in range(H)]
    tmpR = [A.alloc("tmpR_%d" % h, [128], F32) for h in range(H)]
    t1 = [A.alloc("t1_%d" % i, [TB], F32) for i in range(2)]
    t2 = [A.alloc("t2_%d" % i, [TB], F32) for i in range(2)]
    for h in range(H):
        P.memset("dve", R32[h], 0.0)
        P.memset("dve", Rbf[h], 0.0)
    maskT = K["maskT_f"]
    ident = K["ident_b"]
    ci = [0, 0]

    def chunk(h, tb, tt):
        q, k = QT[h][tb % 2], KT[h][tb % 2]
        v, g, ktm = V[h][tb % 2], G[h][tb % 2], Ktm[h][tb % 2]
        oT = outT[h][tb % 2]
        sl = slice(tt * 128, (tt + 1) * 128)
        i2 = ci[h] % 2
        ci[h] += 1
        pAT = PS[4 + 2 * h][:, 0:128]
        pU = PS[4 + 2 * h][:, 128:256]
        py = PS[5 + 2 * h][:, 0:128]
        po = PS[5 + 2 * h][:, 128:256]
        P.mm(pAT, k[:, sl], q[:, sl])
        yield
        am = ATm[h][i2]
        P.tt("dve", am, pAT, maskT, ALU.mult)
        yield
        P.mm(py, am, v[:, tt, :].k(tt), start=True, stop=False)
        P.mm(py, q[:, sl], Rbf[h], start=False, stop=True)
        P.mm(pU, ktm[:, tt, :].k(tt), v[:, tt, :].k(tt))
        yield
        P.op("dve", lambda e: e.bn_stats(st[h][i2].ap, py.ap), [py], [st[h][i2]])
        P.tt("dve", tmpR[h], pU, R32[h], ALU.add)
        yield
        P.op("dve", lambda e: e.bn_aggr(mv[h][i2].ap, st[h][i2].ap), [st[h][i2]], [mv[h][i2]])
        P.ts("dve", R32[h], tmpR[h], gCs[h], None, ALU.mult)
        P.ts("dve", Rbf[h], tmpR[h], gCs[h], None, ALU.mult)
        yield
        P.act(tc_[h][i2], mv[h][i2][:, 1:2], AF.Ln, bias=HN_EPS)
        yield
        P.act(rs[h][i2], tc_[h][i2], AF.Exp, scale=-0.5)
        yield
        P.ts("dve", yn[h][i2], py, mv[h][i2][:, 0:1], rs[h][i2], ALU.subtract, ALU.mult)
        yield
        P.tt("dve", yn[h][i2], yn[h][i2], gn_bcs[h], ALU.mult)
        P.tt("dve", ob[h][i2], yn[h][i2], g[:, tt, :].k(tt), ALU.mult)
        yield
        P.mm(po, ob[h][i2], ident)
        yield
        P.copy("dve", oT[:, sl].k(tt), po)

    for tb in range(nblk):
        t0 = tb * TB
        x = xb[tb % 2]
        xblk(x, tb)
        bg_emit(BG_PER_SLOT)
        for h in range(H):
            tab = tb4[h][tb % 2]
            P.dma("sp", tab, T(tabs2[h].ap[:, :, t0:t0 + TB].rearrange("f p t -> p f t"), None))
            for which, dst in ((0, QT[h][tb % 2]), (1, KT[h][tb % 2])):
                pa, pb = PS[which * 2], PS[which * 2 + 1]
                proj_fm(P, pa, ws[h][which * 2], x)
                a1, a2 = t1[which], t2[which]
                P.act(a1, pa, AF.Copy)
                P.mm(pb, K["pm_r"], a1)
                P.tt("dve", a2, pb, tab[:, which * 2 + 1, :], ALU.mult)
                P.tt("dve", a1, a1, tab[:, which * 2, :], ALU.mult)
                P.tt("dve", dst, a1, a2, ALU.add)
            v, g, ktm = V[h][tb % 2], G[h][tb % 2], Ktm[h][tb % 2]
            k = KT[h][tb % 2]
            for tt in range(4):
                pv = PS[tt % 2]
                proj_tm(P, pv[:, 0:256], wvg[h], x, tt)
                P.copy("dve", v[:, tt, :].k(tt), pv[:, 0:128])
                P.act(g[:, tt, :].k(tt), pv[:, 128:256], AF.Silu)
            for tt in range(4):
                pTt = PS[2 + tt % 2][:, 0:128]
                P.mm(pTt, k[:, tt * 128:(tt + 1) * 128], ident)
                P.copy("dve", ktm[:, tt, :].k(tt), pTt)
        for tt in range(4):
            gens = [chunk(h, tb, tt) for h in range(H)]
            alive = list(gens)
            while alive:
                for gnr in list(alive):
                    try:
                        next(gnr)
                    except StopIteration:
                        alive.remove(gnr)
        for h in range(H):
            oT = outT[h][tb % 2]
            P.dma("sp", T(out_rows2[h].ap[:, t0:t0 + TB], out_rows2[h].key), oT,
                  rkeys=[oT[:, 0:1].k(tt) for tt in range(4)])


NSP = 1306
MIXW = 5632


def mixer_cols(g):
    cols = []
    ar = np.arange(128)
    perm_r = (ar + 64) % 128
    perm_d = ar.copy()
    perm_d[0:16] = ar[0:16] + 16
    perm_d[16:32] = ar[16:32] - 16
    for r in range(2):
        h = 2 * g + r
        q0, k0, v0, g0 = 0 + h * 128, 512 + h * 128, 1024 + h * 128, 1536 + h * 128
        cols += [q0 + ar, q0 + perm_r, k0 + ar, k0 + perm_r, v0 + ar, g0 + ar]
    for r in range(2):
        h = 2 * g + r
        for base in (2048, 3072):
            for m in range(2):
                b0 = base + h * 256 + m * 128
                cols += [b0 + ar, b0 + perm_d]
        cols += [4096 + h * 256 + np.arange(256)]
    for r in range(2):
        h = 2 * g + r
        cols += [5120 + h * 128 + ar, 5632 + h * 128 + ar,
                 np.full(128, 7168 + h), np.full(128, 7172 + h),
                 6144 + h * 128 + ar, 6656 + h * 128 + ar]
    return np.concatenate(cols)


def mixed_order(g):
    o = []
    for r in range(2):
        o.append((2 * g + r) * 128 + np.arange(128))
    for r in range(2):
        o.append(512 + (2 * g + r) * 256 + np.arange(256))
    for r in range(2):
        o.append(1536 + (2 * g + r) * 128 + np.arange(128))
    return np.concatenate(o)


def rope_np(S, rot_dim, theta):
    pos = np.arange(S, dtype=np.float32)
    inv = (np.float32(theta) ** (-np.arange(0, rot_dim, 2, dtype=np.float32) / np.float32(rot_dim))).astype(np.float32)
    ang = (pos[:, None] * inv[None, :]).astype(np.float32)
    return np.cos(ang).astype(np.float32), np.sin(ang).astype(np.float32)


def ret_tables(g, S):
    cos, sin = rope_np(S, 128, 10000.0)
    c = np.concatenate([cos, cos], axis=1).T
    s = np.concatenate([-sin, sin], axis=1).T
    out = np.zeros((2, 4, 128, S), np.float32)
    i = (np.arange(S) % 128).astype(np.float64)
    for r in range(2):
        h = 2 * g + r
        lg = math.log(1.0 - 2.0 ** (-5.0 - h))
        dq = np.exp((i + 1.0) * lg)
        dk = np.exp(-(i + 1.0) * lg) * (128.0 ** -0.5)
        out[r, 0] = c * dq
        out[r, 1] = s * dq
        out[r, 2] = c * dk
        out[r, 3] = s * dk
    return out


def ret_gC(h):
    return float(math.exp(128.0 * math.log(1.0 - 2.0 ** (-5.0 - h))))


def diff_tables(S):
    cos, sin = rope_np(S, 32, 500000.0)
    c = np.ones((128, S), np.float32)
    s = np.zeros((128, S), np.float32)
    c[0:16] = cos.T
    c[16:32] = cos.T
    s[0:16] = -sin.T
    s[16:32] = sin.T
    return np.stack([c, s])


def tile_wgu(w):
    L = w.shape[0]
    return np.ascontiguousarray(w.reshape(L, NDC, 128, 2, NFC, 128).transpose(0, 4, 3, 2, 1, 5)).reshape(L, NFC, 2, 128, NDC * 128)


def tile_wd(w):
    L, Kd = w.shape[0], w.shape[1]
    nk = Kd // 128
    return np.ascontiguousarray(w.reshape(L, nk, 128, NDC, 128).transpose(0, 3, 2, 1, 4)).reshape(L, NDC, 128, nk * 128)


def const_tables():
    ar = np.arange(128)
    maskT = (ar[None, :] >= ar[:, None]).astype(np.float32)
    x = np.arange(896)
    dmask = ((x[None, :] - 384 - ar[:, None]) >= 0).astype(np.float32)
    perm_r = (ar + 64) % 128
    perm_d = ar.copy()
    perm_d[0:16] = ar[0:16] + 16
    perm_d[16:32] = ar[16:32] - 16
    pm_r = (ar[:, None] == perm_r[None, :]).astype(np.float32)
    pm_d = (ar[:, None] == perm_d[None, :]).astype(np.float32)
    return {"ident_f": np.eye(128, dtype=np.float32), "maskT_f": maskT, "dmask": dmask, "pm_r": pm_r, "pm_d": pm_d}


def small_params(g, l, ret_norm_g, diff_lambda, diff_norm_g, conv_w, conv_b, gate_b, mlstm_norm_g):
    sp = np.zeros((128, NSP), np.float32)
    one = np.ones((128, 1), np.float32)
    sp[:, 0:256] = one * ret_norm_g[l][2 * g * 128:(2 * g + 2) * 128][None, :]
    sp[:, 256:512] = one * diff_norm_g[l][None, :]
    sp[:, 512:768] = one * mlstm_norm_g[l][2 * g * 128:(2 * g + 2) * 128][None, :]
    sp[:, 768:1280] = one * diff_lambda[l].reshape(1, 512)
    for r in range(2):
        h = 2 * g + r
        b = 1280 + r * 10
        sp[:, b:b + 4] = conv_w[l][:, h * 128:(h + 1) * 128].T
        sp[:, b + 4:b + 8] = conv_w[l][:, 512 + h * 128:512 + (h + 1) * 128].T
        sp[:, b + 8] = conv_b[l][h * 128:(h + 1) * 128]
        sp[:, b + 9] = conv_b[l][512 + h * 128:512 + (h + 1) * 128]
        sp[:, 1300 + r * 2] = gate_b[l][0, h]
        sp[:, 1300 + r * 2 + 1] = gate_b[l][1, h]
        sp[:, 1304 + r] = ret_gC(h)
    return sp


def mixer_phase(P, A, PS, K, xblk, wm, rtab, dtab, spd, mixedT, S, g, lam_init, sp, lamc):
    P.fence()
    P.dma("sp", sp, spd)
    P.tt("dve", sp[:, 768:896], sp[:, 768:896], sp[:, 896:1024], ALU.mult)
    P.tt("dve", sp[:, 1024:1152], sp[:, 1024:1152], sp[:, 1152:1280], ALU.mult)
    P.rsum(lamc[:, 0:1], sp[:, 768:896])
    P.rsum(lamc[:, 1:2], sp[:, 1024:1152])
    P.act(lamc[:, 2:4], lamc[:, 0:2], AF.Exp)
    P.tt("dve", lamc[:, 4:5], lamc[:, 3:4], lamc[:, 2:3], ALU.subtract)
    P.ts("dve", lamc[:, 5:6], lamc[:, 4:5], -lam_init, None, ALU.add)
    P.ts("dve", sp[:, 256:512], sp[:, 256:512], 1.0 - lam_init, None, ALU.mult)
    neglam = lamc[:, 5:6]
    ret_stage2(P, A, PS, K, xblk, wm, [0, 768], [T(rtab.ap[r], None) for r in range(2)],
               [sp[:, r * 128:(r + 1) * 128] for r in range(2)],
               [T(mixedT.ap[r * 128:(r + 1) * 128, :], mixedT.key) for r in range(2)], S,
               [sp[:, 1304 + r:1305 + r] for r in range(2)])
    for r in range(2):
        diff_stage(P, A, PS, K, xblk, wm, 1536 + r * 1280, dtab, sp[:, 256:512], neglam,
                   T(mixedT.ap[256 + r * 256:256 + (r + 1) * 256, :], mixedT.key), S)
    for r in range(2):
        mlstm_stage(P, A, PS, K, xblk, wm, 4096 + r * 768, sp[:, 1280 + r * 10:1290 + r * 10],
                    sp[:, 1300 + r * 2:1302 + r * 2], sp[:, 512 + r * 128:512 + (r + 1) * 128],
                    T(mixedT.ap[768 + r * 128:768 + (r + 1) * 128, :], mixedT.key), S)


ARENA_BYTES = 196 * 1024
import ml_dtypes
NP_BF16 = ml_dtypes.bfloat16


def _setup_consts(P, cdram, need_mixer):
    K = {}
    K["ones"] = P.sbuf("ones", [128, 128], F32)
    P.memset("dve", K["ones"], 1.0)
    K["ident_f"] = P.sbuf("ident_f", [128, 128], F32)
    P.dma("sp", K["ident_f"], cdram["ident_f"])
    if need_mixer:
        K["maskT_f"] = P.sbuf("maskT_f", [128, 128], F32)
        for nm in ("pm_r", "pm_d"):
            K[nm] = P.sbuf(nm, [128, 128], F32)
            P.dma("sp", K[nm], cdram[nm])
        K["ident_b"] = P.sbuf("ident_b", [128, 128], BF16)
        K["dmask"] = P.sbuf("dmask", [128, 896], BF16)
        P.dma("sp", K["maskT_f"], cdram["maskT_f"])
        P.dma("pool", K["ident_b"], cdram["ident_f"])
        P.dma("pool", K["dmask"], cdram["dmask"])
    return K


def ln_cols(ln_g, ln_b, l, k):
    return np.concatenate([ln_g[l, k].reshape(16, 128).T, ln_b[l, k].reshape(16, 128).T], axis=1).astype(np.float32)


def kernel(**inputs):
    return run_model_fused(inputs, 4, 4096)


PAIRS = [[0, 1], [2, 3], [4, 5], [6, 7]]


def outproj_phase_fused(P, A, PS, CONST, mg_bf, gsel, xres_in, wo, lcol, xT_out, xres_out, NT):
    P.fence()
    A.reset()
    cand = [A.alloc("cand%d" % i, [NDC, TB], BF16) for i in range(2)]
    z = A.alloc("z", [NDC, TB], F32)
    ob = A.alloc("ob", [NDC, TB], BF16)
    wds, sq, mu, rstd, tmp = ln_bufs(A, NDC)
    bufs = (wds, sq, mu, rstd, tmp, ob, None)
    for blk in range(NT // TB):
        t0 = blk * TB
        for h in range(2):
            for r in range(2):
                for k in range(4):
                    src = mg_bf.ap[k, r * 256:(r + 1) * 256, h * NT + t0:h * NT + t0 + TB].rearrange("(c p) t -> p c t", p=128)
                    c0 = r * 8 + k * 2
                    P.dma("sp", cand[h][:, c0:c0 + 2, :].k(c0), T(src, mg_bf.key))
        for c0 in range(0, NDC, 2):
            a = cand[0][:, c0:c0 + 2, :].k(c0)
            b_ = cand[1][:, c0:c0 + 2, :].k(c0)
            P.ts("dve", b_, b_, gsel[:, 1:2], None, ALU.mult)
            P.op("dve", lambda e, a=a.ap, b=b_.ap, s_=gsel.ap[:, 0:1]: e.scalar_tensor_tensor(a, a, s_, b, ALU.mult, ALU.add),
                 [a, b_, gsel], [a, cand[0]])
        down_ln(P, A, PS, CONST, cand[0], NDC, wo, lcol, z, xres_in, xT_out, xres_out, None, t0, bufs)


def build_fused_program(S):
    NT = S // 2
    nbh = NT // TB
    nc = bass.Bass("TRN2", target_bir_lowering=False)
    P = Prog(nc)
    EI = "ExternalInput"
    cdram = {k: P.dram("c_" + k, list(v.shape), F32, kind=EI) for k, v in const_tables().items()}
    K = _setup_consts(P, cdram, True)
    x_tok = P.dram("x_tok", [NT, D], F32, kind=EI)
    wgu1 = P.dram("ffn1_gu", [DEPTH, NFC, 2, 128, NDC * 128], F32, kind=EI)
    wd1 = P.dram("ffn1_d", [DEPTH, NDC, 128, FF], F32, kind=EI)
    wgu2 = P.dram("ffn2_gu", [DEPTH, NFC, 2, 128, NDC * 128], F32, kind=EI)
    wd2 = P.dram("ffn2_d", [DEPTH, NDC, 128, FF], F32, kind=EI)
    wo = P.dram("wo", [DEPTH, NDC, 128, D], F32, kind=EI)
    wm = P.dram("wm", [DEPTH, D, MIXW], F32, kind=EI)
    rtab = P.dram("rtab", [2, 4, 128, S], F32, kind=EI)
    dtab = P.dram("dtab", [2, 128, S], F32, kind=EI)
    spd = P.dram("spd", [DEPTH, 128, NSP], F32, kind=EI)
    lnd = P.dram("lnc", [128, 32 * 3 * DEPTH], F32, kind=EI)
    gseld = P.dram("gsel", [128, 2], F32, kind=EI)
    out_tok = P.dram("out_tok", [NT, D], F32, kind="ExternalOutput")
    lcol = P.sbuf("lcol", [128, 32 * 3 * DEPTH], F32)
    P.dma("sp", lcol, lnd)
    gsel = P.sbuf("gselc", [128, 2], F32)
    P.dma("sp", gsel, gseld)
    sp = P.sbuf("sp", [128, NSP], F32)
    lamc = P.sbuf("lamc", [128, 8], F32)
    lc = lambda l, k: (lcol[:, 32 * (3 * l + k):32 * (3 * l + k) + 16], lcol[:, 32 * (3 * l + k) + 16:32 * (3 * l + k) + 32])
    A = Arena(P, ARENA_BYTES)
    PS = [P.psum("ps%d" % i, [128, 512], F32) for i in range(8)]
    xs32 = P.dram("xs32", [D, NT // 2], F32)
    xg32 = P.dram("xg32", [4, 2 * 512, NT // 2], F32)
    mx32 = P.dram("mx32", [1024, S // 2], F32)
    mg32 = P.dram("mg32", [4, 2 * 256, S // 2], F32)
    xs_bf = T(xs32.ap.bitcast(BF16), xs32.key)
    xg_bf = T(xg32.ap.bitcast(BF16), xg32.key)
    mx_bf = T(mx32.ap.bitcast(BF16), mx32.key)
    mg_bf = T(mg32.ap.bitcast(BF16), mg32.key)
    r_a = P.dram("r_a", [D, NT], F32)
    r_b = P.dram("r_b", [D, NT], F32)
    t_a = P.dram("t_a", [D, NT], BF16)
    t_b = P.dram("t_b", [D, NT], BF16)

    def xblk(x, tb):
        h, t0l = tb // nbh, (tb % nbh) * TB
        for k in range(4):
            src = xg_bf.ap[k, h * 512:(h + 1) * 512, t0l:t0l + TB].rearrange("(c p) t -> p c t", p=128)
            P.dma("sp", x[:, k * 4:(k + 1) * 4, :], T(src, xg_bf.key))

    wcache = [(P.dram("wgu_bf%d" % i, [NFC, 2, 128, NDC * 128], BF16), P.dram("wd_bf%d" % i, [NDC, 128, FF], BF16))
              for i in range(2 * DEPTH)]
    prologue_phase(P, A, PS, K, x_tok, t_a, r_a, NT)
    cur_T, cur_r = t_a, r_a
    for l in range(DEPTH):
        lam_init = 0.8 - 0.6 * math.exp(-0.3 * l)
        ffn_phase(P, A, PS, K, cur_T, cur_r, T(wgu1.ap[l], None), T(wd1.ap[l], None), lc(l, 0), xs_bf, r_b, NT,
                  wgu_bf=wcache[2 * l][0], wd_bf=wcache[2 * l][1], cached=False)
        for k in range(4):
            P.cc("AllGather", PAIRS, T(xg32.ap[k], xg32.key), T(xs32.ap[k * 512:(k + 1) * 512, :], xs32.key))
        mixer_phase(P, A, PS, K, xblk, T(wm.ap[l], None), rtab, dtab, T(spd.ap[l], None), mx_bf, S, 0, lam_init, sp, lamc)
        for k in range(4):
            P.cc("AllGather", PAIRS, T(mg32.ap[k], mg32.key), T(mx32.ap[k * 256:(k + 1) * 256, :], mx32.key))
        outproj_phase_fused(P, A, PS, K, mg_bf, gsel, r_b, T(wo.ap[l], None), lc(l, 1), t_a, r_a, NT)
        if l == DEPTH - 1:
            ffn_phase(P, A, PS, K, t_a, r_a, T(wgu2.ap[l], None), T(wd2.ap[l], None), lc(l, 2), None, None, NT, out_tok=out_tok,
                      wgu_bf=wcache[2 * l + 1][0], wd_bf=wcache[2 * l + 1][1], cached=False)
        else:
            ffn_phase(P, A, PS, K, t_a, r_a, T(wgu2.ap[l], None), T(wd2.ap[l], None), lc(l, 2), t_b, r_b, NT,
                      wgu_bf=wcache[2 * l + 1][0], wd_bf=wcache[2 * l + 1][1], cached=False)
            cur_T, cur_r = t_b, r_b
            t_a, t_b = t_b, t_a
            r_a, r_b = r_b, r_a
            cur_T, cur_r = t_a, r_a
    P.emit()
    return nc


def run_model_fused(inp, B, S):
    f32 = lambda a: np.ascontiguousarray(np.asarray(a, dtype=np.float32))
    x = f32(inp["x"])
    NT = S // 2
    ncores = 2 * B
    consts = {"c_" + k: v for k, v in const_tables().items()}
    ln_g, ln_b = f32(inp["ln_g"]), f32(inp["ln_b"])
    w_in, w_out = f32(inp["w_in"]), f32(inp["w_out"])
    depth = w_in.shape[0]
    gorder = np.concatenate([mixed_order(0), mixed_order(1)])
    dtab = diff_tables(S)
    lnc = np.concatenate([ln_cols(ln_g, ln_b, l, k) for l in range(depth) for k in range(3)], axis=1)
    shared = {"ffn1_gu": tile_wgu(f32(inp["ffn1_w_gu"])), "ffn1_d": tile_wd(f32(inp["ffn1_w_down"])),
              "ffn2_gu": tile_wgu(f32(inp["ffn2_w_gu"])), "ffn2_d": tile_wd(f32(inp["ffn2_w_down"])),
              "wo": tile_wd(np.ascontiguousarray(w_out[:, gorder, :])), "dtab": dtab, "lnc": np.ascontiguousarray(lnc)}
    shared.update(consts)
    per_g = []
    for g in range(2):
        sel = np.zeros((128, 2), np.float32)
        sel[:, g] = 1.0
        per_g.append({"wm": np.ascontiguousarray(w_in[:, :, mixer_cols(g)]), "rtab": ret_tables(g, S), "gsel": sel,
                      "spd": np.stack([small_params(g, l, f32(inp["ret_norm_g"]), f32(inp["diff_lambda"]),
                                                    f32(inp["diff_norm_g"]), f32(inp["mlstm_conv_w"]),
                                                    f32(inp["mlstm_conv_b"]), f32(inp["mlstm_gate_b"]),
                                                    f32(inp["mlstm_norm_g"])) for l in range(depth)])})
    maps = []
    for c in range(ncores):
        b, g = c // 2, c % 2
        m = dict(shared)
        m.update(per_g[g])
        m["x_tok"] = np.ascontiguousarray(x[b, g * NT:(g + 1) * NT, :])
        maps.append(m)
    nc = build_fused_program(S)
    res = run_bass_kernel_spmd(nc, maps, core_ids=list(range(ncores)))
    out = np.zeros((B, S, D), np.float32)
    for c in range(ncores):
        b, g = c // 2, c % 2
        out[b, g * NT:(g + 1) * NT, :] = res.results[c]["out_tok"]
    return out
```
